# Optimizing a Trainium2 kernel written in Bass

```python
import math
import jax, jax.numpy as jnp
from jax import lax
import numpy as np

D_MODEL = 1024
BATCH = 32
SEQ = 2048
DEPTH = 4

GRID_W = 64
CTX_LEN = 256
N_MIXERS = 3
N_A = (DEPTH + 2) // 3
N_B = (DEPTH + 1) // 3
N_C = DEPTH // 3
NORM_EPS = 1e-6
ROPE_BASE = 10000.0
Q_BLOCK = 128
NEG_INF = -1e30
MOD_GAIN = 0.5

A_HEADS = 8
A_Q_RANK = 512
A_KV_RANK = 256
A_NOPE = 128
A_ROPE = 64
A_VDIM = 128
A_WIDTH = A_HEADS * A_VDIM
A_IN = A_Q_RANK + A_KV_RANK + A_ROPE + A_WIDTH

B_HEADS = 16
B_KV_HEADS = 4
B_GROUP = B_HEADS // B_KV_HEADS
B_HDIM = 64
B_WINDOW = 128
B_BLOCK = 128
B_WIDTH = B_HEADS * B_HDIM
B_KVW = B_KV_HEADS * B_HDIM
B_IN = B_WIDTH + 2 * B_KVW + B_WIDTH

C_WIDTH = D_MODEL
C_ORDER = 2
C_SHORT = 3
C_BANDS = 16
C_EMB = 1 + 2 * C_BANDS
C_FFN = 64
C_TARGET = 1e-2
C_FAST = 0.3
C_SLOW = 1.5
C_MIN_DECAY = math.log(C_TARGET) / C_SLOW
C_MAX_DECAY = math.log(C_TARGET) / C_FAST
C_FILTER_GAIN = 0.1
C_IN = (C_ORDER + 1) * C_WIDTH + C_WIDTH

kernel_name = 'hybrid_mla_swa_hyena_prefix_dit'


def rmsnorm(x, g):
    xf = x.astype(jnp.float32)
    y = xf * lax.rsqrt(jnp.mean(xf * xf, axis=-1, keepdims=True) + NORM_EPS)
    return (y * g.astype(jnp.float32)).astype(x.dtype)


def axial_rope(n, rot_dim):
    rows = n // GRID_W
    row = jnp.repeat(jnp.arange(rows, dtype=jnp.float32), GRID_W)
    col = jnp.tile(jnp.arange(GRID_W, dtype=jnp.float32), rows)
    per_axis = rot_dim // 2
    inv = ROPE_BASE ** (-jnp.arange(0, per_axis, 2, dtype=jnp.float32) / per_axis)
    ang = jnp.concatenate([row[:, None] * inv, col[:, None] * inv], axis=-1)
    return jnp.cos(ang), jnp.sin(ang)


def apply_rope(x, cos, sin):
    half = x.shape[-1] // 2
    x1 = x[..., :half].astype(jnp.float32)
    x2 = x[..., half:].astype(jnp.float32)
    return jnp.concatenate([x1 * cos - x2 * sin, x1 * sin + x2 * cos], axis=-1).astype(x.dtype)


def dense_attend(q, k, v, scale):
    s = jnp.einsum('bqhd,bkhd->bhqk', q, k).astype(jnp.float32) * scale
    p = jax.nn.softmax(s, axis=-1).astype(v.dtype)
    return jnp.einsum('bhqk,bkhd->bqhd', p, v)


def blockwise_attend(q, k, v, scale):
    b, n, h, d = q.shape
    qb = q.reshape(b, n // Q_BLOCK, Q_BLOCK, h, d).swapaxes(0, 1)
    ob = lax.map(lambda qi: dense_attend(qi, k, v, scale), qb)
    return ob.swapaxes(0, 1).reshape(b, n, h, v.shape[-1])


def sink_attend(scores, values, sink_l):
    sink_col = jnp.broadcast_to(sink_l[None, :, :, None, None], scores[0].shape[:-1] + (1,))
    p = jax.nn.softmax(jnp.concatenate(scores + [sink_col], axis=-1), axis=-1)
    offset = 0
    parts = []
    for s, v in zip(scores, values):
        kk = s.shape[-1]
        parts.append(jnp.einsum('bhgqk,bkhd->bqhgd', p[..., offset:offset + kk].astype(v.dtype), v))
        offset += kk
    out = parts[0]
    for part in parts[1:]:
        out = out + part
    return out


def mla_mixer(hx, hc, w_in, g_q, w_q, g_kv, w_kv, w_out, with_ctx_out):
    b, n, _ = hx.shape
    cos, sin = axial_rope(n, A_ROPE)
    split_at = [A_Q_RANK, A_Q_RANK + A_KV_RANK, A_Q_RANK + A_KV_RANK + A_ROPE]
    scale = (A_NOPE + A_ROPE) ** -0.5

    def queries(cq, rope):
        bb, m, _ = cq.shape
        q = (rmsnorm(cq, g_q) @ w_q).reshape(bb, m, A_HEADS, A_NOPE + A_ROPE)
        if rope:
            q = jnp.concatenate([q[..., :A_NOPE], apply_rope(q[..., A_NOPE:], cos[:, None, :], sin[:, None, :])], axis=-1)
        return q

    def keys_values(ckv, kr, rope):
        bb, m, _ = ckv.shape
        kv = (rmsnorm(ckv, g_kv) @ w_kv).reshape(bb, m, A_HEADS, A_NOPE + A_VDIM)
        if rope:
            kr = apply_rope(kr, cos, sin)
        kr = jnp.broadcast_to(kr[:, :, None, :], (bb, m, A_HEADS, A_ROPE))
        return jnp.concatenate([kv[..., :A_NOPE], kr], axis=-1), kv[..., A_NOPE:]

    cq_x, ckv_x, kr_x, gate_x = jnp.split(hx @ w_in, split_at, axis=-1)
    q_x = queries(cq_x, True)
    k_x, v_x = keys_values(ckv_x, kr_x, True)
    if with_ctx_out:
        cq_c, ckv_c, kr_c, gate_c = jnp.split(hc @ w_in, split_at, axis=-1)
    else:
        ckv_c, kr_c = jnp.split(hc @ w_in[:, split_at[0]:split_at[2]], [A_KV_RANK], axis=-1)
    k_c, v_c = keys_values(ckv_c, kr_c, False)

    k_all = jnp.concatenate([k_c, k_x], axis=1)
    v_all = jnp.concatenate([v_c, v_x], axis=1)
    o_x = blockwise_attend(q_x, k_all, v_all, scale)
    y_x = (o_x.reshape(b, n, A_WIDTH) * jax.nn.silu(gate_x)) @ w_out
    if not with_ctx_out:
        return y_x, None
    q_c = queries(cq_c, False)
    o_c = dense_attend(q_c, k_c, v_c, scale)
    y_c = (o_c.reshape(b, hc.shape[1], A_WIDTH) * jax.nn.silu(gate_c)) @ w_out
    return y_x, y_c


def swa_mixer(hx, hc, w_in, sink, w_out, with_ctx_out):
    b, n, _ = hx.shape
    cos, sin = axial_rope(n, B_HDIM)
    scale = B_HDIM ** -0.5
    sink_l = sink.astype(jnp.float32).reshape(B_KV_HEADS, B_GROUP)

    def project(h):
        m = h.shape[1]
        q, k, v, gate = jnp.split(h @ w_in, [B_WIDTH, B_WIDTH + B_KVW, B_WIDTH + 2 * B_KVW], axis=-1)
        return (q.reshape(b, m, B_KV_HEADS, B_GROUP, B_HDIM), k.reshape(b, m, B_KV_HEADS, B_HDIM),
                v.reshape(b, m, B_KV_HEADS, B_HDIM), gate)

    q_x, k_x, v_x, gate_x = project(hx)
    q_c, k_c, v_c, gate_c = project(hc)
    q_x = apply_rope(q_x, cos[:, None, None, :], sin[:, None, None, :])
    k_x = apply_rope(k_x, cos[:, None, :], sin[:, None, :])

    pad = ((0, 0), (B_BLOCK, B_BLOCK), (0, 0), (0, 0))
    kp = jnp.pad(k_x, pad)
    vp = jnp.pad(v_x, pad)
    n_blocks = n // B_BLOCK
    qb = q_x.reshape(b, n_blocks, B_BLOCK, B_KV_HEADS, B_GROUP, B_HDIM).swapaxes(0, 1)

    def band_block(args):
        i, qi = args
        start = i * B_BLOCK
        kb = lax.dynamic_slice_in_dim(kp, start, 3 * B_BLOCK, axis=1)
        vb = lax.dynamic_slice_in_dim(vp, start, 3 * B_BLOCK, axis=1)
        qpos = start + jnp.arange(B_BLOCK)
        kpos = start - B_BLOCK + jnp.arange(3 * B_BLOCK)
        mask = (jnp.abs(qpos[:, None] - kpos[None, :]) <= B_WINDOW) & (kpos[None, :] >= 0) & (kpos[None, :] < n)
        s_band = jnp.einsum('bqhgd,bkhd->bhgqk', qi, kb).astype(jnp.float32) * scale
        s_band = jnp.where(mask, s_band, NEG_INF)
        s_ctx = jnp.einsum('bqhgd,bkhd->bhgqk', qi, k_c).astype(jnp.float32) * scale
        return sink_attend([s_ctx, s_band], [v_c, vb], sink_l)

    ob = lax.map(band_block, (jnp.arange(n_blocks), qb))
    o_x = ob.swapaxes(0, 1).reshape(b, n, B_WIDTH)
    y_x = (o_x * jax.nn.silu(gate_x)) @ w_out
    if not with_ctx_out:
        return y_x, None
    s_cc = jnp.einsum('bqhgd,bkhd->bhgqk', q_c, k_c).astype(jnp.float32) * scale
    o_c = sink_attend([s_cc], [v_c], sink_l).reshape(b, hc.shape[1], B_WIDTH)
    y_c = (o_c * jax.nn.silu(gate_c)) @ w_out
    return y_x, y_c


def short_conv(u, w, bias):
    n = u.shape[1]
    p = C_SHORT // 2
    up = jnp.pad(u, ((0, 0), (p, p), (0, 0)))
    y = bias
    for j in range(C_SHORT):
        y = y + up[:, j:j + n] * w[j]
    return y


def hyena_filters(n, f_w1, f_b1, f_freq, f_w2, f_b2, f_w3):
    t = jnp.linspace(0.0, 1.0, n, dtype=jnp.float32)[:, None]
    w = (2.0 * math.pi / n) * jnp.arange(n, dtype=jnp.float32)[:, None]
    bands = jnp.linspace(1e-4, C_BANDS - 1, C_BANDS, dtype=jnp.float32)[None, :]
    z = jnp.concatenate([t, jnp.cos(bands * w), -jnp.sin(bands * w)], axis=-1)
    freq = f_freq.astype(jnp.float32)
    h = jnp.sin(freq * (z @ f_w1 + f_b1))
    h = jnp.sin(freq * (h @ f_w2 + f_b2))
    h = (h @ f_w3).astype(jnp.float32).reshape(n, 2, C_ORDER, C_WIDTH)
    deltas = jnp.abs(jnp.linspace(C_MIN_DECAY, C_MAX_DECAY, C_WIDTH, dtype=jnp.float32))
    return h * jnp.exp(-t[:, :, None, None] * deltas)


def bidir_long_conv(u, h_fwd, h_bwd, bias):
    n = u.shape[1]
    lag0 = h_fwd[:1] + h_bwd[:1]
    k = jnp.concatenate([lag0, h_fwd[1:], jnp.zeros((1, C_WIDTH), jnp.float32), h_bwd[:0:-1]], axis=0)
    uf = jnp.fft.rfft(u.astype(jnp.float32), n=2 * n, axis=1)
    kf = jnp.fft.rfft(k, axis=0)
    y = jnp.fft.irfft(uf * kf[None], n=2 * n, axis=1)[:, :n]
    return (y + u.astype(jnp.float32) * bias.astype(jnp.float32)).astype(u.dtype)


def hyena_mixer(hx, hc, w_in, conv_w, conv_b, f_w1, f_b1, f_freq, f_w2, f_b2, f_w3, filt_bias, w_out, with_ctx_out):
    def run(h):
        n = h.shape[1]
        p = h @ w_in
        u = short_conv(p[..., :(C_ORDER + 1) * C_WIDTH], conv_w, conv_b)
        gate = p[..., (C_ORDER + 1) * C_WIDTH:]
        v, *gx = jnp.split(u, C_ORDER + 1, axis=-1)
        filt = hyena_filters(n, f_w1, f_b1, f_freq, f_w2, f_b2, f_w3)
        z = v
        for o in range(C_ORDER):
            z = gx[o] * bidir_long_conv(z, filt[:, 0, o], filt[:, 1, o], filt_bias[o])
        return (z * jax.nn.silu(gate)) @ w_out
    y_x = run(hx)
    y_c = run(hc) if with_ctx_out else None
    return y_x, y_c


def setup_inputs(seed: int = 0) -> dict:
    key = jax.random.key(seed)
    ks = iter(jax.random.split(key, 40))
    f32 = jnp.float32

    def nrm(shape, fan_in, gain=1.0):
        return jax.random.normal(next(ks), shape, f32) * (gain * fan_in ** -0.5)

    def gains(shape):
        return 1.0 + 0.01 * jax.random.normal(next(ks), shape, f32)

    def small(shape, s=0.01):
        return s * jax.random.normal(next(ks), shape, f32)

    D = D_MODEL
    return {
        'x': jax.random.normal(next(ks), (BATCH, SEQ, D), f32),
        'c': jax.random.normal(next(ks), (BATCH, D), f32),
        'ctx': jax.random.normal(next(ks), (BATCH, CTX_LEN, D), f32),
        'c_ctx': jax.random.normal(next(ks), (D,), f32),
        'w_mod': nrm((DEPTH, D, 3 * D), D, MOD_GAIN),
        'b_mod': small((DEPTH, 3 * D)),
        'g_pre': gains((DEPTH, D)),
        'g_post': gains((DEPTH, D)),
        'a_w_in': nrm((N_A, D, A_IN), D),
        'a_g_q': gains((N_A, A_Q_RANK)),
        'a_w_q': nrm((N_A, A_Q_RANK, A_HEADS * (A_NOPE + A_ROPE)), A_Q_RANK),
        'a_g_kv': gains((N_A, A_KV_RANK)),
        'a_w_kv': nrm((N_A, A_KV_RANK, A_HEADS * (A_NOPE + A_VDIM)), A_KV_RANK),
        'a_w_out': nrm((N_A, A_WIDTH, D), A_WIDTH),
        'b_w_in': nrm((N_B, D, B_IN), D),
        'b_sink': small((N_B, B_HEADS), 0.5),
        'b_w_out': nrm((N_B, B_WIDTH, D), B_WIDTH),
        'c_w_in': nrm((N_C, D, C_IN), D),
        'c_conv_w': nrm((N_C, C_SHORT, (C_ORDER + 1) * C_WIDTH), C_SHORT),
        'c_conv_b': small((N_C, (C_ORDER + 1) * C_WIDTH)),
        'c_f_w1': nrm((N_C, C_EMB, C_FFN), C_EMB),
        'c_f_b1': small((N_C, C_FFN), 0.1),
        'c_f_freq': gains((N_C, C_FFN)),
        'c_f_w2': nrm((N_C, C_FFN, C_FFN), C_FFN),
        'c_f_b2': small((N_C, C_FFN), 0.1),
        'c_f_w3': nrm((N_C, C_FFN, 2 * C_ORDER * C_WIDTH), C_FFN, C_FILTER_GAIN),
        'c_filt_bias': small((N_C, C_ORDER, C_WIDTH), 0.1),
        'c_w_out': nrm((N_C, C_WIDTH, D), C_WIDTH),
    }


def reference(x, c, ctx, c_ctx, w_mod, b_mod, g_pre, g_post,
              a_w_in, a_g_q, a_w_q, a_g_kv, a_w_kv, a_w_out,
              b_w_in, b_sink, b_w_out,
              c_w_in, c_conv_w, c_conv_b, c_f_w1, c_f_b1, c_f_freq, c_f_w2, c_f_b2, c_f_w3, c_filt_bias, c_w_out):
    cond_x = jax.nn.silu(c)
    cond_c = jax.nn.silu(c_ctx)
    for layer in range(DEPTH):
        kind = layer % N_MIXERS
        j = layer // N_MIXERS
        with_ctx_out = layer < DEPTH - 1
        mx = cond_x @ w_mod[layer] + b_mod[layer]
        mc = cond_c @ w_mod[layer] + b_mod[layer]
        shift_x, scale_x, gate_x = jnp.split(mx[:, None, :], 3, axis=-1)
        shift_c, scale_c, gate_c = jnp.split(mc, 3, axis=-1)
        hx = rmsnorm(x, g_pre[layer]) * (1.0 + scale_x) + shift_x
        hc = rmsnorm(ctx, g_pre[layer]) * (1.0 + scale_c) + shift_c
        if kind == 0:
            yx, yc = mla_mixer(hx, hc, a_w_in[j], a_g_q[j], a_w_q[j], a_g_kv[j], a_w_kv[j], a_w_out[j], with_ctx_out)
        elif kind == 1:
            yx, yc = swa_mixer(hx, hc, b_w_in[j], b_sink[j], b_w_out[j], with_ctx_out)
        else:
            yx, yc = hyena_mixer(hx, hc, c_w_in[j], c_conv_w[j], c_conv_b[j], c_f_w1[j], c_f_b1[j], c_f_freq[j],
                                 c_f_w2[j], c_f_b2[j], c_f_w3[j], c_filt_bias[j], c_w_out[j], with_ctx_out)
        x = x + gate_x * rmsnorm(yx, g_post[layer])
        if with_ctx_out:
            ctx = ctx + gate_c * rmsnorm(yc, g_post[layer])
    return x
```

```python
import math
from contextlib import ExitStack
import numpy as np
import ml_dtypes
import concourse.bass as bass
import concourse.mybir as mybir
from concourse.bass_utils import run_bass_kernel_spmd

F32 = mybir.dt.float32
BF16 = mybir.dt.bfloat16
AF = mybir.ActivationFunctionType
ALU = mybir.AluOpType

D = 1024
SEQ = 2048
CTX = 256
NT = SEQ + CTX
BATCH = 32
DEPTH = 4
EPS = 1e-6
NCORES = 8


class Prog:
    def __init__(self, nc, ndma=24):
        self.nc = nc
        self.engs = ["pe", "act", "dve", "pool", "sp"]
        self.ops = {e: [] for e in self.engs}
        self.cnt = {e: 0 for e in self.engs}
        self.tw = {}
        self.tr = {}
        self.seen = {e: {} for e in self.engs}
        self.ndma = ndma
        self.dcnt = [0] * ndma
        self.drr = 0

    def _waits(self, eng, r, w, is_dma=False):
        raw = {}
        oth = {}
        for t in r:
            x = self.tw.get(t)
            if x:
                raw[x[0]] = max(raw.get(x[0], 0), x[1])
        for t in w:
            x = self.tw.get(t)
            if x:
                oth[x[0]] = max(oth.get(x[0], 0), x[1])
            for sk, v in self.tr.get(t, {}).items():
                oth[sk] = max(oth.get(sk, 0), v)
        res = {}
        for sk, v in raw.items():
            if sk == eng and eng == "pe":
                continue
            res[sk] = max(res.get(sk, 0), v)
        for sk, v in oth.items():
            if sk == eng and eng == "pe":
                continue
            res[sk] = max(res.get(sk, 0), v)
        out = []
        for sk, v in res.items():
            if self.seen[eng].get(sk, 0) >= v:
                continue
            self.seen[eng][sk] = v
            out.append((sk, v))
        return out

    def _done(self, key, val, r, w):
        for t in w:
            self.tw[t] = (key, val)
            self.tr[t] = {}
        for t in r:
            d = self.tr.setdefault(t, {})
            d[key] = max(d.get(key, 0), val)

    def op(self, eng, fn, r=(), w=()):
        for k, v in self._waits(eng, r, w):
            self.ops[eng].append(("w", k, v))
        self.cnt[eng] += 1
        self.ops[eng].append(("o", fn, eng, 1))
        self._done(eng, self.cnt[eng], r, w)

    def dma(self, q, out, in_, r=(), w=()):
        i = self.drr
        self.drr = (self.drr + 1) % self.ndma
        key = ("d", i)
        for k, v in self._waits(q, r, w, is_dma=True):
            self.ops[q].append(("w", k, v))
        prev = self.dcnt[i] * 16
        if prev and self.seen[q].get(key, 0) < prev:
            self.seen[q][key] = prev
            self.ops[q].append(("w", key, prev))
        self.dcnt[i] += 1
        self.ops[q].append(("o", (lambda e, o=out, i_=in_: e.dma_start(out=o, in_=i_)), key, 16))
        self._done(key, self.dcnt[i] * 16, r, w)

    def barrier(self, full=True):
        for e in self.engs:
            if not full and e == "pool":
                continue
            for k in ["pe", "act", "dve", "pool"]:
                if k != e and self.cnt[k] > self.seen[e].get(k, 0):
                    self.seen[e][k] = self.cnt[k]
                    self.ops[e].append(("w", k, self.cnt[k]))
            for i in range(self.ndma):
                v = self.dcnt[i] * 16
                if v > self.seen[e].get(("d", i), 0):
                    self.seen[e][("d", i)] = v
                    self.ops[e].append(("w", ("d", i), v))
        if full:
            self.tw = {}
            self.tr = {}

    def emit(self, es):
        nc = self.nc
        sems = {}
        for e in ["pe", "act", "dve", "pool"]:
            sems[e] = es.enter_context(nc.semaphore("s_" + e))
        for i in range(self.ndma):
            sems[("d", i)] = es.enter_context(nc.semaphore("sd%d" % i))
        block = es.enter_context(nc.Block())

        def run(name):
            def f(eng):
                for it in self.ops[name]:
                    if it[0] == "w":
                        eng.wait_ge(sems[it[1]], it[2])
                    else:
                        it[1](eng).then_inc(sems[it[2]], it[3])
            return f

        block.tensor(run("pe"))
        block.scalar(run("act"))
        block.vector(run("dve"))
        block.gpsimd(run("pool"))
        block.sync(run("sp"))


TILES = [(0, 256, True)] + [(256 + 512 * i, 512, False) for i in range(4)]


def build(NB, NL):
    nc = bass.Bass("TRN2", target_bir_lowering=False)
    es = ExitStack()
    p = Prog(nc)

    def din(name, shape, dt=F32):
        return nc.dram_tensor(name, list(shape), dt, kind="ExternalInput").ap()

    xT = din("xT", [NB, 8, 128, SEQ])
    ctxT = din("ctxT", [NB, 8, 128, CTX])
    cT = din("cT", [128, 8, NB + 1])
    w_mod = din("w_mod", [4, D, 3 * D])
    b_modT = din("b_modT", [4, 128, 24])
    g_preT = din("g_preT", [4, 128, 8])
    g_postT = din("g_postT", [4, 128, 8])
    a_w_in = din("a_w_in", [2, D, 1920])
    a_g_qT = din("a_g_qT", [2, 128, 4])
    a_w_q = din("a_w_q", [2, 512, 8 * 256])
    a_g_kvT = din("a_g_kvT", [2, 128, 2])
    a_w_kv = din("a_w_kv", [2, 256, 2048])
    a_w_out = din("a_w_out", [2, D, D])
    b_wA = din("b_wA", [4, D, 512])
    b_wB = din("b_wB", [4, D, 576])
    b_w_out = din("b_w_out", [1, D, D])
    sinkT = din("sinkT", [128, 8])
    maskT = din("maskT", [128, 384])
    c_w_in = din("c_w_in", [1, D, 4096])
    c_w_out = din("c_w_out", [1, D, D])
    convT = din("convT", [128, 24, 4])
    f_w1 = din("f_w1", [33, 64])
    f_w2 = din("f_w2", [64, 64])
    f_w3 = din("f_w3", [64, 4096])
    f_sm = din("f_sm", [64, 3])
    f_bias = din("f_bias", [1, 2048])
    zT_x = din("zT_x", [33, SEQ])
    zT_c = din("zT_c", [33, CTX])
    dec_x = din("dec_x", [SEQ, D])
    dec_c = din("dec_c", [CTX, D])
    FT_x = din("FT_x", [SEQ, 2 * SEQ], BF16)
    GT_x = din("GT_x", [2 * SEQ, SEQ], BF16)
    FT_c = din("FT_c", [CTX, 2 * CTX], BF16)
    GT_c = din("GT_c", [2 * CTX, CTX], BF16)
    identT = din("identT", [128, 128])
    KFD_x = nc.dram_tensor("kfd_x", [2, 2, 8, 128, SEQ], F32, kind="Internal").ap()
    KFD_c = nc.dram_tensor("kfd_c", [2, 2, 8, 128, CTX], F32, kind="Internal").ap()
    GD = nc.dram_tensor("gd_scr", [3, 8, 128, NT], BF16, kind="Internal").ap()
    cosT = din("cosT", [128, SEQ])
    sinT = din("sinT", [128, SEQ])
    outT = nc.dram_tensor("outT", [NB, 8, 128, SEQ], F32, kind="ExternalOutput").ap()
    RES = nc.dram_tensor("res_scr", [8, 128, NT], F32, kind="Internal").ap()

    def sb(name, shape, dt):
        return es.enter_context(nc.sbuf_tensor(name, list(shape), dt))

    HT = sb("HT", [128, 8, NT], BF16)
    BIG = sb("BIG", [128, 16, NT], BF16)
    XIN = [sb("XIN%d" % i, [128, 8, 512], F32) for i in range(2)]
    SQ = sb("SQ", [128, 8, 512], BF16)
    RS = [sb("RS%d" % i, [128, 512], F32) for i in range(2)]
    TMP = [sb("TMP%d" % i, [128, 512], F32) for i in range(4)]
    PT = [sb("PT%d" % i, [128, 512], BF16) for i in range(4)]
    DSUM = sb("DSUM", [128, 512], BF16)
    WB = [sb("WB%d" % i, [128, 8, 576], BF16) for i in range(2)]
    WQH = [sb("WQH%d" % i, [128, 4, 256], BF16) for i in range(2)]
    WKVH = [sb("WKVH%d" % i, [128, 2, 256], BF16) for i in range(2)]
    ONES = sb("ONES", [128, 128], BF16)
    ONLO = sb("ONLO", [128, 128], BF16)
    ONHI = sb("ONHI", [128, 128], BF16)
    MASK = sb("MASK", [128, 384], BF16)
    ES = sb("ES", [128, 8], F32)
    EPSC = sb("EPSC", [128, 1], F32)
    COS = sb("COS", [128, SEQ], BF16)
    SIN = sb("SIN", [128, SEQ], BF16)
    IDENT = sb("IDENT", [128, 128], BF16)
    FCA = sb("FCA", [128, 2, 256], BF16)
    FCB = sb("FCB", [128, 2, 256], BF16)
    GCT = sb("GCT", [128, 4, 256], BF16)
    KN = sb("KN", [128, 2, 2, 8], F32)
    CONVW = sb("CONVW", [128, 24, 4], F32)
    CT = sb("CT", [128, 8, NB + 1], F32)
    CS = sb("CS", [128, 8, NB + 1], BF16)
    MODR = sb("MODR", [128, 24, NB + 1], F32)
    MODS = sb("MODS", [128, 4, 3, 8, NB + 1], F32)
    BM = sb("BM", [128, 4, 24], F32)
    GPRE = sb("GPRE", [128, 4, 8], F32)
    GPOST = sb("GPOST", [128, 4, 8], F32)
    AGQ = sb("AGQ", [128, 2, 4], F32)
    AGKV = sb("AGKV", [128, 2, 2], F32)
    PS = [es.enter_context(nc.psum_tensor("PS%d" % i, [128, 512], F32)) for i in range(8)]
    S_B = [0, 1]
    O_B = [2, 3]
    D_B = [4, 5]
    PJ = [6, 7]
    rot = {"yb": 0, "pj": 0, "tmp": 0, "rs": 0, "wb": 0, "xin": 0, "pt": 0, "s": 0, "od": 0, "wh": 0}

    def nxt(k, n):
        v = rot[k]
        rot[k] = (v + 1) % n
        return v

    def mm(out, lhsT, rhs, start, stop, r, w):
        p.op("pe", lambda e: e.matmul(out, lhsT, rhs, start=start, stop=stop), r=r, w=w)

    def act(out, in_, func, r, w, bias=None, scale=None):
        kw = {}
        if bias is not None:
            kw["bias"] = bias
        if scale is not None:
            kw["scale"] = scale
        p.op("act", lambda e: e.activation(out=out, in_=in_, func=func, **kw), r=r, w=w)

    def tt(out, a, b, op, r, w, eng="dve"):
        p.op(eng, lambda e: e.tensor_tensor(out, a, b, op), r=r, w=w)

    def tsc(out, a, s1, s2, op0, op1, r, w, eng="dve"):
        if op1 is None:
            p.op(eng, lambda e: e.tensor_scalar(out, a, s1, None, op0), r=r, w=w)
        else:
            p.op(eng, lambda e: e.tensor_scalar(out, a, s1, s2, op0, op1), r=r, w=w)

    def stt(out, a, s, b, op0, op1, r, w, eng="dve"):
        p.op(eng, lambda e: e.scalar_tensor_tensor(out, a, s, b, op0, op1), r=r, w=w)

    def vcopy(out, in_, r, w):
        p.op("dve", lambda e: e.tensor_copy(out, in_), r=r, w=w)

    def rstd_from(psb, n, k_feats, rtoks):
        i = nxt("rs", 2)
        rs = RS[i]
        act(rs[:, :n], PS[psb][:, :n], AF.Ln, scale=1.0 / k_feats, bias=EPSC[:, 0:1], r=[("PS", psb), ("EPSC",)] + rtoks, w=[("RS", i)])
        act(rs[:, :n], rs[:, :n], AF.Exp, scale=-0.5, r=[("RS", i)], w=[("RS", i)])
        return rs, ("RS", i)

    def sumsq(src3, nch, n, rtoks):
        act(SQ[:, 0:nch, :n], src3, AF.Square, r=rtoks, w=[("SQ",)])
        b = PJ[nxt("pj", 2)]
        for c in range(nch):
            mm(PS[b][:, :n], ONES[:, :], SQ[:, c, :n], c == 0, c == nch - 1, r=[("SQ",), ("ONES",)], w=[("PS", b)])
        return b

    def load_w(dst, src, wtok):
        p.dma("pool", dst, src, r=[("W16R",)], w=[wtok])

    W16 = {}

    def precast(name, src, lead, rows, cols):
        t16 = nc.dram_tensor(name + "_16", [lead, rows, cols], BF16, kind="Internal").ap()
        W16[name] = t16
        return [(t16[i, r0:r0 + 128, :], src[i, r0:r0 + 128, :]) for i in range(lead) for r0 in range(0, rows, 128)]

    p.op("dve", lambda e: e.memset(ONES[:, :], 1.0), w=[("ONES",)])
    p.op("dve", lambda e: e.memset(EPSC[:, :], EPS), w=[("EPSC",)])
    p.op("dve", lambda e: e.memset(ONLO[:, :], 0.0), w=[("ONES",)])
    p.op("dve", lambda e: e.memset(ONHI[:, :], 0.0), w=[("ONES",)])
    p.op("dve", lambda e: e.memset(ONLO[:, 0:64], 1.0), w=[("ONES",)])
    p.op("dve", lambda e: e.memset(ONHI[:, 64:128], 1.0), w=[("ONES",)])
    p.dma("pool", MASK[:, :], maskT[:, :], w=[("MASK",)])
    p.dma("sp", ES[:, :], sinkT[:, :], w=[("ES",)])
    act(ES[:, :], ES[:, :], AF.Exp, r=[("ES",)], w=[("ES",)])
    p.dma("pool", COS[:, :], cosT[:, :], w=[("COS",)])
    p.dma("pool", SIN[:, :], sinT[:, :], w=[("SIN",)])
    p.dma("pool", IDENT[:, :], identT[:, :], w=[("IDENT",)])
    p.dma("sp", FCA[:, :, :], FT_c[:, 0:256].rearrange("(k p) f -> p k f", p=128), w=[("FC",)])
    p.dma("sp", FCB[:, :, :], FT_c[:, 256:512].rearrange("(k p) f -> p k f", p=128), w=[("FC",)])
    p.dma("sp", GCT[:, :, :], GT_c.rearrange("(k p) t -> p k t", p=128), w=[("FC",)])
    p.dma("sp", CONVW[:, :, :], convT[:, :, :], w=[("CONVW",)])
    p.dma("sp", CT[:, :, :], cT[:, :, :], w=[("CT",)])
    p.dma("sp", BM[:, :, :], b_modT.rearrange("l p c -> p l c"), w=[("BM",)])
    p.dma("sp", GPRE[:, :, :], g_preT.rearrange("l p c -> p l c"), w=[("GP",)])
    p.dma("sp", GPOST[:, :, :], g_postT.rearrange("l p c -> p l c"), w=[("GP",)])
    p.dma("sp", AGQ[:, :, :], a_g_qT.rearrange("l p c -> p l c"), w=[("GP",)])
    p.dma("sp", AGKV[:, :, :], a_g_kvT.rearrange("l p c -> p l c"), w=[("GP",)])
    act(CS[:, :, :], CT[:, :, :], AF.Silu, r=[("CT",)], w=[("CS",)])
    NC1 = NB + 1
    for l in range(NL):
        b = PJ[nxt("pj", 2)]
        for pc in range(6):
            wi = nxt("wb", 2)
            load_w(WB[wi][:, :, 0:512], w_mod[l, :, pc * 512:(pc + 1) * 512].rearrange("(c p) m -> p c m", p=128), ("WB", wi))
            for mi in range(4):
                m = pc * 4 + mi
                for kc in range(8):
                    mm(PS[b][:, m * NC1:(m + 1) * NC1], WB[wi][:, kc, mi * 128:(mi + 1) * 128], CS[:, kc, :],
                       kc == 0, kc == 7, r=[("WB", wi), ("CS",)], w=[("PS", b)])
        for jcol in range(NC1):
            tt(MODR[:, :, jcol], PS[b][:, 0:24 * NC1].rearrange("p (m j) -> p m j", j=NC1)[:, :, jcol], BM[:, l, :], ALU.add,
               r=[("PS", b), ("BM",)], w=[("MODR",)])
        tsc(MODR[:, 8:16, :], MODR[:, 8:16, :], 1.0, None, ALU.add, None, r=[("MODR",)], w=[("MODR",)])
        for jcol in range(NC1):
            tt(MODS[:, l, 0, :, jcol], MODR[:, 8:16, jcol], GPRE[:, l, :], ALU.mult, r=[("MODR",), ("GP",)], w=[("MODS",)])
            tt(MODS[:, l, 2, :, jcol], MODR[:, 16:24, jcol], GPOST[:, l, :], ALU.mult, r=[("MODR",), ("GP",)], w=[("MODS",)])
        p.op("dve", lambda e, l=l: e.tensor_copy(MODS[:, l, 1, :, :], MODR[:, 0:8, :]), r=[("MODR",)], w=[("MODS",)])
    jobs = []
    jobs += precast("a_w_in", a_w_in, 2, D, 1920)
    jobs += precast("a_w_q", a_w_q, 2, 512, 2048)
    jobs += precast("a_w_kv", a_w_kv, 2, 256, 2048)
    jobs += precast("a_w_out", a_w_out, 2, D, D)
    jobs += precast("b_wA", b_wA, 4, D, 512)
    jobs += precast("b_wB", b_wB, 4, D, 576)
    jobs += precast("b_w_out", b_w_out, 1, D, D)
    jobs += precast("c_w_in", c_w_in, 1, D, 4096)
    jobs += precast("c_w_out", c_w_out, 1, D, D)
    n_first = 2 * 8 + 2 * 4 + 2 * 2 + 2 * 8
    for (dst_, src_) in jobs[:n_first]:
        p.dma("pool", dst_, src_, w=[("W16R",)])
    a_w_in, a_w_q, a_w_kv, a_w_out = W16["a_w_in"], W16["a_w_q"], W16["a_w_kv"], W16["a_w_out"]
    b_wA, b_wB, b_w_out = W16["b_wA"], W16["b_wB"], W16["b_w_out"]
    c_w_in, c_w_out = W16["c_w_in"], W16["c_w_out"]
    p.barrier(full=False)
    for (dst_, src_) in jobs[n_first:]:
        p.dma("pool", dst_, src_, w=[("W16R",)])

    def res_src(l, b, t0, n, is_ctx):
        if l == 0:
            if is_ctx:
                return ctxT[b].rearrange("c p t -> p c t")[:, :, t0:t0 + n]
            return xT[b].rearrange("c p t -> p c t")[:, :, t0 - CTX:t0 - CTX + n]
        return RES.rearrange("c p t -> p c t")[:, :, t0:t0 + n]

    def norm_stats(xi, n):
        bk = sumsq(XIN[xi][:, :, :n], 8, n, [("XIN", xi)])
        return rstd_from(bk, n, D, [])

    def modulate(l, b, ti, xi, rs, rtok, chunks=range(8)):
        t0, n, is_ctx = TILES[ti]
        col = NB if is_ctx else b
        for c in chunks:
            k = nxt("tmp", 4)
            stt(TMP[k][:, :n], XIN[xi][:, c, :n], MODS[:, l, 0, c, col:col + 1], rs[:, :n], ALU.mult, ALU.mult,
                r=[("XIN", xi), rtok, ("MODS",)], w=[("TMP", k)])
            if c < 5:
                act(HT[:, c, t0:t0 + n], TMP[k][:, :n], AF.Identity, bias=MODS[:, l, 1, c, col:col + 1],
                    r=[("TMP", k), ("MODS",)], w=[("HT", c, ti)])
            else:
                tsc(HT[:, c, t0:t0 + n], TMP[k][:, :n], MODS[:, l, 1, c, col:col + 1], None, ALU.add, None,
                    r=[("TMP", k), ("MODS",)], w=[("HT", c, ti)], eng="pool")

    def prenorm(l, b):
        def stats(ti):
            t0, n, is_ctx = TILES[ti]
            xi = nxt("xin", 2)
            p.dma("sp", XIN[xi][:, :, :n], res_src(l, b, t0, n, is_ctx), r=[("RES", ti)], w=[("XIN", xi)])
            rs, rtok = norm_stats(xi, n)
            return xi, rs, rtok

        cur = stats(0)
        for ti in range(len(TILES)):
            nx = stats(ti + 1) if ti + 1 < len(TILES) else None
            modulate(l, b, ti, *cur)
            cur = nx

    def post(l, b, gsrc, gtok, w_out_ap, with_ctx, r0):
        last = (l == NL - 1)
        region = BIG[:, r0:r0 + 8, :].rearrange("p c t -> p (c t)")
        WO = region[:, 0:8 * D].rearrange("p (c m) -> p c m", m=D)
        YBS = [region[:, 8192 + y * 4096:8192 + (y + 1) * 4096].rearrange("p (c t) -> p c t", t=512) for y in range(2)]
        for half in range(2):
            load_w(WO[:, :, half * 512:(half + 1) * 512], w_out_ap[:, half * 512:(half + 1) * 512].rearrange("(c p) m -> p c m", p=128), ("WO", half))
        tiles = [ti for ti, (t0, n, is_ctx) in enumerate(TILES) if not (is_ctx and not with_ctx)]
        SB2 = [2, 3]

        def stream_A(ti):
            t0, n, is_ctx = TILES[ti]
            xi = nxt("xin", 2)
            ybi = nxt("yb", 2)
            YB = YBS[ybi]
            p.dma("sp", XIN[xi][:, :, :n], res_src(l, b, t0, n, is_ctx), r=[("RES", ti)], w=[("XIN", xi)])

            def grp(m):
                def f():
                    bk = PJ[nxt("pj", 2)]
                    for kc in range(8):
                        mm(PS[bk][:, :n], WO[:, kc, m * 128:(m + 1) * 128], gsrc(kc, t0, n), kc == 0, kc == 7,
                           r=[("WO", m // 4), gtok(kc, ti)], w=[("PS", bk)])
                    act(YB[:, m, :n], PS[bk][:, :n], AF.Copy, r=[("PS", bk)], w=[("YB", ybi, m)])
                    act(SQ[:, m, :n], PS[bk][:, :n], AF.Square, r=[("PS", bk)], w=[("SQ",)])
                return f
            return (xi, ybi), [grp(m) for m in range(8)]

        def stream_B(ti, xi, ybi):
            t0, n, is_ctx = TILES[ti]
            col = NB if is_ctx else b
            YB = YBS[ybi]
            st = {}

            def g0():
                bk = SB2[0]
                for c in range(8):
                    mm(PS[bk][:, :n], ONES[:, :], SQ[:, c, :n], c == 0, c == 7, r=[("SQ",), ("ONES",)], w=[("PS", bk)])
                st["rs"] = rstd_from(bk, n, D, [])

            def gm(m0):
                def f():
                    rs, rtok = st["rs"]
                    for m in (m0, m0 + 1):
                        k = nxt("tmp", 4)
                        tt(TMP[k][:, :n], YB[:, m, :n], rs[:, :n], ALU.mult, r=[("YB", ybi, m), rtok], w=[("TMP", k)])
                        stt(XIN[xi][:, m, :n], TMP[k][:, :n], MODS[:, l, 2, m, col:col + 1], XIN[xi][:, m, :n], ALU.mult, ALU.add,
                            r=[("TMP", k), ("XIN", xi), ("MODS",)], w=[("XIN", xi)])
                return f

            def g5():
                if last:
                    if not is_ctx:
                        p.dma("sp", outT[b].rearrange("c p t -> p c t")[:, :, t0 - CTX:t0 - CTX + n], XIN[xi][:, :, :n],
                              r=[("XIN", xi)], w=[("OUT", ti)])
                    return
                p.dma("sp", RES.rearrange("c p t -> p c t")[:, :, t0:t0 + n], XIN[xi][:, :, :n], r=[("XIN", xi)], w=[("RES", ti)])
                act(YB[:, :, :n], XIN[xi][:, :, :n], AF.Square, r=[("XIN", xi)], w=[("YB", ybi, m) for m in range(8)])
                bk = SB2[1]
                for c in range(8):
                    mm(PS[bk][:, :n], ONES[:, :], YB[:, c, :n], c == 0, c == 7, r=[("YB", ybi, c), ("ONES",)], w=[("PS", bk)])
                st["rs2"] = rstd_from(bk, n, D, [])

            def g67(chunks):
                def f():
                    if not last:
                        rs2, rtok2 = st["rs2"]
                        modulate(l + 1, b, ti, xi, rs2, rtok2, chunks)
                return f
            return [g0, gm(0), gm(2), gm(4), gm(6), g5, g67(range(0, 4)), g67(range(4, 8))]

        prevB = None
        for ti in tiles:
            info, thA = stream_A(ti)
            if prevB is None:
                for fa in thA:
                    fa()
            else:
                for fa, fb in zip(thA, prevB):
                    fb()
                    fa()
            prevB = stream_B(ti, *info)
        for fb in prevB:
            fb()

    def copy_through(l, b):
        pass

    def rope_combine(dst, bA, bB, t0, n, rows, rtoks, wtok):
        s0 = t0 - CTX
        k1 = nxt("tmp", 4)
        k2 = nxt("tmp", 4)
        tt(TMP[k1][rows, :n], PS[bA][rows, :n], COS[rows, s0:s0 + n], ALU.mult, r=[("PS", bA), ("COS",)] + rtoks, w=[("TMP", k1)])
        tt(TMP[k2][rows, :n], PS[bB][rows, :n], SIN[rows, s0:s0 + n], ALU.mult, r=[("PS", bB), ("SIN",)] + rtoks, w=[("TMP", k2)])
        tt(dst, TMP[k1][rows, :n], TMP[k2][rows, :n], ALU.add, r=[("TMP", k1), ("TMP", k2)], w=[wtok])

    def mla(l, j, b, with_ctx):
        scale = (128 + 64) ** -0.5
        win = a_w_in[j]
        for pc in range(4):
            ncols = 512 if pc < 3 else 384
            wi = nxt("wb", 2)
            load_w(WB[wi][:, :, :ncols], win[:, pc * 512:pc * 512 + ncols].rearrange("(c p) m -> p c m", p=128), ("WB", wi))
            for mi in range(ncols // 128):
                m = pc * 4 + mi
                for ti, (t0, n, is_ctx) in enumerate(TILES):
                    if m == 6:
                        bA = PJ[nxt("pj", 2)]
                        bB = PJ[nxt("pj", 2)]
                        for kc in range(8):
                            mm(PS[bA][0:64, :n], WB[wi][:, kc, mi * 128:mi * 128 + 64], HT[:, kc, t0:t0 + n], kc == 0, kc == 7,
                               r=[("WB", wi), ("HT", kc, ti)], w=[("PS", bA)])
                        if is_ctx:
                            act(BIG[0:64, 6, t0:t0 + n], PS[bA][0:64, :n], AF.Copy, r=[("PS", bA)], w=[("KR", ti)])
                            continue
                        for kc in range(8):
                            mm(PS[bB][0:64, :n], WB[wi][:, kc, mi * 128 + 64:mi * 128 + 128], HT[:, kc, t0:t0 + n], kc == 0, kc == 7,
                               r=[("WB", wi), ("HT", kc, ti)], w=[("PS", bB)])
                        rope_combine(BIG[0:64, 6, t0:t0 + n], bA, bB, t0, n, slice(0, 64), [], ("KR", ti))
                        continue
                    bk = PJ[nxt("pj", 2)]
                    for kc in range(8):
                        mm(PS[bk][:, :n], WB[wi][:, kc, mi * 128:(mi + 1) * 128], HT[:, kc, t0:t0 + n], kc == 0, kc == 7,
                           r=[("WB", wi), ("HT", kc, ti)], w=[("PS", bk)])
                    if m < 6:
                        act(BIG[:, m, t0:t0 + n], PS[bk][:, :n], AF.Copy, r=[("PS", bk)], w=[("CQ", m, ti)])
                    else:
                        act(BIG[:, m + 1, t0:t0 + n], PS[bk][:, :n], AF.Silu, r=[("PS", bk)], w=[("G", m - 7, ti)])
        for ti, (t0, n, is_ctx) in enumerate(TILES):
            for (c0, nch, gt, kf) in ((0, 4, AGQ, 512), (4, 2, AGKV, 256)):
                if c0 == 0 and is_ctx and not with_ctx:
                    continue
                bk = sumsq(BIG[:, c0:c0 + nch, t0:t0 + n], nch, n, [("CQ", c0 + i, ti) for i in range(nch)])
                rs, rtok = rstd_from(bk, n, kf, [])
                for c in range(nch):
                    stt(BIG[:, c0 + c, t0:t0 + n], BIG[:, c0 + c, t0:t0 + n], gt[:, j, c:c + 1], rs[:, :n], ALU.mult, ALU.mult,
                        r=[("CQ", c0 + c, ti), rtok, ("GP",)], w=[("CQ", c0 + c, ti)])
        p.barrier(full=False)
        p.op("dve", lambda e: e.memset(HT[64:128, 3, :], 0.0), w=[("QR", ti_) for ti_ in range(5)])
        p.op("dve", lambda e: e.memset(BIG[64:128, 6, :], 0.0), w=[("KR", ti_) for ti_ in range(5)])
        QN = HT[:, 0, :]
        KN = HT[:, 1, :]
        V = HT[:, 2, :].rearrange("p (k d) -> p k d", d=128)
        QR = HT[:, 3, :]
        KR = BIG[:, 6, :]
        def load_head(hh):
            w_ = nxt("wh", 2)
            load_w(WQH[w_][:, :, :], a_w_q[j, :, hh * 256:(hh + 1) * 256].rearrange("(c p) m -> p c m", p=128), ("WQH", w_))
            load_w(WKVH[w_][:, :, :], a_w_kv[j, :, hh * 256:(hh + 1) * 256].rearrange("(c p) m -> p c m", p=128), ("WKVH", w_))
            return w_

        wh_next = load_head(0)
        for h in range(8):
            wh = wh_next
            for ti, (t0, n, is_ctx) in enumerate(TILES):
                bk = PJ[nxt("pj", 2)]
                for kc in range(2):
                    mm(PS[bk][:, :n], WKVH[wh][:, kc, 0:128], BIG[:, 4 + kc, t0:t0 + n], kc == 0, kc == 1,
                       r=[("WKVH", wh), ("CQ", 4 + kc, ti)], w=[("PS", bk)])
                act(KN[:, t0:t0 + n], PS[bk][:, :n], AF.Copy, r=[("PS", bk)], w=[("KN", ti)])
                bk = PJ[nxt("pj", 2)]
                nk = n // 128
                for kk in range(nk):
                    for kc in range(2):
                        mm(PS[bk][:, kk * 128:(kk + 1) * 128], BIG[:, 4 + kc, t0 + kk * 128:t0 + (kk + 1) * 128], WKVH[wh][:, kc, 128:256],
                           kc == 0, kc == 1, r=[("WKVH", wh), ("CQ", 4 + kc, ti)], w=[("PS", bk)])
                p.op("dve", lambda e, bk=bk, t0=t0, nk=nk, n=n: e.tensor_copy(
                    V[:, t0 // 128:t0 // 128 + nk, :], PS[bk][:, :n].rearrange("p (k d) -> p k d", d=128)),
                    r=[("PS", bk)], w=[("V", ti)])
                if is_ctx and not with_ctx:
                    continue
                bk = PJ[nxt("pj", 2)]
                for kc in range(4):
                    mm(PS[bk][:, :n], WQH[wh][:, kc, 0:128], BIG[:, kc, t0:t0 + n], kc == 0, kc == 3,
                       r=[("WQH", wh), ("CQ", kc, ti)], w=[("PS", bk)])
                act(QN[:, t0:t0 + n], PS[bk][:, :n], AF.Copy, r=[("PS", bk)], w=[("QN", ti)])
                bA = PJ[nxt("pj", 2)]
                for kc in range(4):
                    mm(PS[bA][0:64, :n], WQH[wh][:, kc, 128:192], BIG[:, kc, t0:t0 + n], kc == 0, kc == 3,
                       r=[("WQH", wh), ("CQ", kc, ti)], w=[("PS", bA)])
                if is_ctx:
                    act(QR[0:64, t0:t0 + n], PS[bA][0:64, :n], AF.Copy, r=[("PS", bA)], w=[("QR", ti)])
                else:
                    bB = PJ[nxt("pj", 2)]
                    for kc in range(4):
                        mm(PS[bB][0:64, :n], WQH[wh][:, kc, 192:256], BIG[:, kc, t0:t0 + n], kc == 0, kc == 3,
                           r=[("WQH", wh), ("CQ", kc, ti)], w=[("PS", bB)])
                    rope_combine(QR[0:64, t0:t0 + n], bA, bB, t0, n, slice(0, 64), [], ("QR", ti))
            if h + 1 < 8:
                wh_next = load_head(h + 1)
            for ti, (t0, n, is_ctx) in enumerate(TILES):
                if is_ctx and not with_ctx:
                    continue
                nkc = 2 if is_ctx else 18
                od = nxt("od", 2)
                ob, db = O_B[od], D_B[od]
                def st1(kc):
                    kti = 0 if kc < 2 else 1 + (kc - 2) // 4
                    sbk = (0, 1, 6, 7)[nxt("s", 4)]
                    mm(PS[sbk][:, :n], KN[:, kc * 128:(kc + 1) * 128], QN[:, t0:t0 + n], True, False,
                       r=[("KN", kti), ("QN", ti)], w=[("PS", sbk)])
                    mm(PS[sbk][:, :n], KR[:, kc * 128:(kc + 1) * 128], QR[:, t0:t0 + n], False, True,
                       r=[("KR", kti), ("QR", ti)], w=[("PS", sbk)])
                    pi = nxt("pt", 4)
                    act(PT[pi][:, :n], PS[sbk][:, :n], AF.Exp, scale=scale, r=[("PS", sbk)], w=[("PT", pi)])
                    return (kc, kti, pi)

                def st2(kc, kti, pi):
                    mm(PS[ob][:, :n], V[:, kc, :], PT[pi][:, :n], kc == 0, kc == nkc - 1, r=[("V", kti), ("PT", pi)], w=[("PS", ob)])
                    if kc == 0:
                        vcopy(PS[db][:, :n], PT[pi][:, :n], r=[("PT", pi)], w=[("PS", db)])
                    elif kc == 1:
                        p.op("pool", lambda e, o_=RS[od][:, :n], i_=PT[pi][:, :n]: e.tensor_copy(o_, i_), r=[("PT", pi)], w=[("RS", od)])
                    elif kc % 2 == 0:
                        tt(PS[db][:, :n], PS[db][:, :n], PT[pi][:, :n], ALU.add, r=[("PS", db), ("PT", pi)], w=[("PS", db)])
                    else:
                        tt(RS[od][:, :n], RS[od][:, :n], PT[pi][:, :n], ALU.add, r=[("RS", od), ("PT", pi)], w=[("RS", od)], eng="pool")

                pend = []
                for kc in range(nkc):
                    pend.append(st1(kc))
                    if len(pend) > 2:
                        st2(*pend.pop(0))
                while pend:
                    st2(*pend.pop(0))
                tt(DSUM[:, :n], PS[db][:, :n], RS[od][:, :n], ALU.add, r=[("PS", db), ("RS", od)], w=[("DSUM",)])
                mm(PS[db][:, :n], ONES[:, :], DSUM[:, :n], True, True, r=[("ONES",), ("DSUM",)], w=[("PS", db)])
                k1 = nxt("tmp", 4)
                k2 = nxt("tmp", 4)
                act(TMP[k1][:, :n], PS[db][:, :n], AF.Ln, r=[("PS", db)], w=[("TMP", k1)])
                act(TMP[k1][:, :n], TMP[k1][:, :n], AF.Exp, scale=-1.0, r=[("TMP", k1)], w=[("TMP", k1)])
                tt(TMP[k2][:, :n], PS[ob][:, :n], TMP[k1][:, :n], ALU.mult, r=[("PS", ob), ("TMP", k1)], w=[("TMP", k2)])
                tt(BIG[:, 8 + h, t0:t0 + n], TMP[k2][:, :n], BIG[:, 8 + h, t0:t0 + n], ALU.mult, r=[("TMP", k2), ("G", h, ti)], w=[("G", h, ti)])
        p.barrier()
        post(l, b, lambda kc, t0, n: BIG[:, 8 + kc, t0:t0 + n], lambda kc, ti: ("G", kc, ti), a_w_out[j], with_ctx, 0)

    def swa(l, b, with_ctx):
        scale = 64 ** -0.5
        SS = [0, 1, 2, 3]
        Kt = BIG[:, 2, :]
        VLO = BIG[:, 3, :].rearrange("p (k d) -> p k d", d=128)
        VHI = BIG[:, 4, :].rearrange("p (k d) -> p k d", d=128)
        allv = [("VV", ti) for ti in range(5)]
        p.op("dve", lambda e: e.memset(VLO[:, :, 64:128], 0.0), w=allv)
        p.op("dve", lambda e: e.memset(VHI[:, :, 0:64], 0.0), w=allv)
        for hk in range(4):
            wa = nxt("wb", 2)
            load_w(WB[wa][:, :, 0:512], b_wA[hk].rearrange("(c p) m -> p c m", p=128), ("WB", wa))
            wbi = nxt("wb", 2)
            load_w(WB[wbi][:, :, 0:576], b_wB[hk].rearrange("(c p) m -> p c m", p=128), ("WB", wbi))
            for ti, (t0, n, is_ctx) in enumerate(TILES):
                jobs = [(("Q", qc, ti), BIG[:, qc, t0:t0 + n], wa, qc * 128, 256 + qc * 128) for qc in range(2)]
                jobs.append((("K", ti), Kt[:, t0:t0 + n], wbi, 0, 128))
                for (tok, dst, wsel, oA, oB) in jobs:
                    if tok[0] == "Q" and is_ctx and not with_ctx:
                        continue
                    bA = PJ[nxt("pj", 2)]
                    for kc in range(8):
                        mm(PS[bA][:, :n], WB[wsel][:, kc, oA:oA + 128], HT[:, kc, t0:t0 + n], kc == 0, kc == 7,
                           r=[("WB", wsel), ("HT", kc, ti)], w=[("PS", bA)])
                    if is_ctx:
                        act(dst, PS[bA][:, :n], AF.Copy, r=[("PS", bA)], w=[tok])
                    else:
                        bB = PJ[nxt("pj", 2)]
                        for kc in range(8):
                            mm(PS[bB][:, :n], WB[wsel][:, kc, oB:oB + 128], HT[:, kc, t0:t0 + n], kc == 0, kc == 7,
                               r=[("WB", wsel), ("HT", kc, ti)], w=[("PS", bB)])
                        rope_combine(dst, bA, bB, t0, n, slice(0, 128), [], tok)
                bk = PJ[nxt("pj", 2)]
                nk = n // 128
                for kk in range(nk):
                    for kc in range(8):
                        mm(PS[bk][:, kk * 64:(kk + 1) * 64], HT[:, kc, t0 + kk * 128:t0 + (kk + 1) * 128], WB[wbi][:, kc, 256:320],
                           kc == 0, kc == 7, r=[("WB", wbi), ("HT", kc, ti)], w=[("PS", bk)])
                k0 = t0 // 128
                p.op("dve", lambda e, bk=bk, k0=k0, nk=nk: e.tensor_copy(
                    VLO[:, k0:k0 + nk, 0:64], PS[bk][:, :nk * 64].rearrange("p (k d) -> p k d", d=64)),
                    r=[("PS", bk)], w=[("VV", ti)])
                p.op("dve", lambda e, bk=bk, k0=k0, nk=nk: e.tensor_copy(
                    VHI[:, k0:k0 + nk, 64:128], PS[bk][:, :nk * 64].rearrange("p (k d) -> p k d", d=64)),
                    r=[("PS", bk)], w=[("VV", ti)])
            for qc in range(2):
                c = hk * 2 + qc
                for ti, (t0, n, is_ctx) in enumerate(TILES):
                    if is_ctx and not with_ctx:
                        continue
                    ob, db = 4, 5
                    items = [(0, 0, n, None), (1, 0, n, None)]
                    if not is_ctx:
                        i0 = 4 * (ti - 1)
                        for jb in range(i0 - 1, i0 + 5):
                            if jb < 0 or jb >= 16:
                                continue
                            qlo = max(i0, jb - 1)
                            qhi = min(i0 + 3, jb + 1)
                            items.append((2 + jb, (qlo - i0) * 128, (qhi - i0 + 1) * 128, (qlo - jb + 1) * 128))
                    gb = PJ[nxt("pj", 2)]
                    for kc in range(8):
                        mm(PS[gb][:, :n], WB[wbi][:, kc, 320 + qc * 128:320 + (qc + 1) * 128], HT[:, kc, t0:t0 + n], kc == 0, kc == 7,
                           r=[("WB", wbi), ("HT", kc, ti)], w=[("PS", gb)])
                    kg = nxt("tmp", 4)
                    act(TMP[kg][:, :n], PS[gb][:, :n], AF.Silu, r=[("PS", gb)], w=[("TMP", kg)])
                    def s1(ii, kc, c0, c1, m0):
                        kti = 0 if kc < 2 else 1 + (kc - 2) // 4
                        nn = c1 - c0
                        pts = []
                        for half, rows in ((0, slice(0, 64)), (1, slice(64, 128))):
                            sbk = SS[nxt("s", 4)]
                            mm(PS[sbk][:, :nn], Kt[rows, kc * 128:(kc + 1) * 128], BIG[rows, qc, t0 + c0:t0 + c1], True, True,
                               r=[("K", kti), ("Q", qc, ti)], w=[("PS", sbk)])
                            pi = nxt("pt", 4)
                            act(PT[pi][:, :nn], PS[sbk][:, :nn], AF.Exp, scale=scale, r=[("PS", sbk)], w=[("PT", pi)])
                            if m0 is not None:
                                tt(PT[pi][:, :nn], PT[pi][:, :nn], MASK[:, m0:m0 + nn], ALU.mult, r=[("PT", pi), ("MASK",)], w=[("PT", pi)])
                            pts.append(pi)
                        return (ii, kc, kti, c0, c1, nn, pts)

                    def s2(ii, kc, kti, c0, c1, nn, pts):
                        first = ii == 0
                        lastm = ii == len(items) - 1
                        for half, (vv, oo) in enumerate(((VLO, ONLO), (VHI, ONHI))):
                            pi = pts[half]
                            mm(PS[ob][:, c0:c1], vv[:, kc, :], PT[pi][:, :nn], first and half == 0, lastm and half == 1,
                               r=[("VV", kti), ("PT", pi)], w=[("PS", ob)])
                            mm(PS[db][:, c0:c1], oo[:, :], PT[pi][:, :nn], first and half == 0, lastm and half == 1,
                               r=[("ONES",), ("PT", pi)], w=[("PS", db)])

                    prev = None
                    for ii, it in enumerate(items):
                        cur = s1(ii, *it)
                        if prev is not None:
                            s2(*prev)
                        prev = cur
                    s2(*prev)
                    k1 = nxt("tmp", 4)
                    k2 = nxt("tmp", 4)
                    tsc(TMP[k1][:, :n], PS[db][:, :n], ES[:, c:c + 1], None, ALU.add, None, r=[("PS", db), ("ES",)], w=[("TMP", k1)])
                    act(TMP[k1][:, :n], TMP[k1][:, :n], AF.Ln, r=[("TMP", k1)], w=[("TMP", k1)])
                    act(TMP[k1][:, :n], TMP[k1][:, :n], AF.Exp, scale=-1.0, r=[("TMP", k1)], w=[("TMP", k1)])
                    tt(TMP[k2][:, :n], PS[ob][:, :n], TMP[k1][:, :n], ALU.mult, r=[("PS", ob), ("TMP", k1)], w=[("TMP", k2)])
                    tt(BIG[:, 8 + c, t0:t0 + n], TMP[k2][:, :n], TMP[kg][:, :n], ALU.mult, r=[("TMP", k2), ("TMP", kg)], w=[("G", c, ti)])
        p.barrier()
        post(l, b, lambda kc, t0, n: BIG[:, 8 + kc, t0:t0 + n], lambda kc, ti: ("G", kc, ti), b_w_out[0], with_ctx, 0)

    HTflat = HT[:, :, :].rearrange("p c t -> p (c t)")
    BIGlo = BIG[:, 0:8, :].rearrange("p c t -> p (c t)")
    BIGhi = BIG[:, 8:16, :].rearrange("p c t -> p (c t)")
    Vtok = BIGlo.rearrange("p (k d) -> p k d", d=D)
    KDtok = BIGhi.rearrange("p (k d) -> p k d", d=D)
    XINb = [XIN[i][:, :, :].rearrange("p c t -> p (c t)").bitcast(BF16).rearrange("p (k f) -> p k f", f=512) for i in range(2)]
    KFB = [WB[0][:, :, :].rearrange("p c t -> p (c t)").bitcast(F32)[:, k * 512:(k + 1) * 512] for k in range(4)]
    WB1f = WB[1][:, :, :].rearrange("p c t -> p (c t)")
    GB = [WB1f[:, k * 512:(k + 1) * 512] for k in range(4)]
    YT = [WB1f[:, 2048 + y * 1024:2048 + (y + 1) * 1024].rearrange("p (h f) -> p h f", f=512) for y in range(2)]
    PSbf = [PS[i][:, :].bitcast(BF16) for i in range(8)]
    SS4 = [0, 1, 2, 3]
    TRB = [4, 5]
    rot.update({"kfb": 0, "gb": 0, "yt": 0, "tr": 0, "ost": 0, "ss": 0})

    class G:
        pass
    GX = G()
    GX.gi, GX.n, GX.ntc, GX.tc0, GX.tw, GX.npair, GX.nfh, GX.t0, GX.ti0 = 0, SEQ, 16, 2, 512, 4, 16, CTX, 1
    GX.FT, GX.GT, GX.KFD = FT_x, GT_x, KFD_x
    GX.YA = HTflat[:, 0:16384].rearrange("p (k d) -> p k d", d=D)
    GX.YB = BIGhi[:, 0:16384].rearrange("p (k d) -> p k d", d=D)
    GC = G()
    GC.gi, GC.n, GC.ntc, GC.tc0, GC.tw, GC.npair, GC.nfh, GC.t0, GC.ti0 = 1, CTX, 2, 0, 256, 1, 2, 0, 0
    GC.FT, GC.GT, GC.KFD = FT_c, GT_c, KFD_c
    GC.YA = HTflat[:, 16384:18432].rearrange("p (k d) -> p k d", d=D)
    GC.YB = BIGhi[:, 16384:18432].rearrange("p (k d) -> p k d", d=D)

    def load_F(g, i):
        if g.gi == 1:
            return FCA, FCB
        p.dma("sp", XINb[0][:, :, :], g.FT[:, i * 512:(i + 1) * 512].rearrange("(k p) f -> p k f", p=128), w=[("FA",)])
        p.dma("sp", XINb[1][:, :, :], g.FT[:, SEQ + i * 512:SEQ + (i + 1) * 512].rearrange("(k p) f -> p k f", p=128), w=[("FB",)])
        return XINb[0], XINb[1]

    def fwd_banks(g, FA, FB, srcA, srcB, c, tokA, tokB):
        bA = SS4[nxt("ss", 4)]
        bB = SS4[nxt("ss", 4)]
        for (bk, F, src, ftok, stok) in ((bA, FA, srcA, ("FA",), tokA), (bB, FB, srcB, ("FB",), tokB)):
            for tc in range(g.ntc):
                mm(PS[bk][:, :g.tw], src[:, g.tc0 + tc, c * 128:(c + 1) * 128], F[:, tc, :g.tw], tc == 0, tc == g.ntc - 1,
                   r=[ftok, ("FC",), stok], w=[("PS", bk)])
        return bA, bB

    def transposes_to(src_ap_fn, nblk, dst3, rtoks, wtok, eng="act"):
        trb = TRB[nxt("tr", 2)]
        for blk in range(nblk):
            p.op("pe", lambda e, trb=trb, blk=blk: e.transpose(PSbf[trb][:, blk * 128:(blk + 1) * 128], src_ap_fn(blk), IDENT[:, :]),
                 r=rtoks + [("IDENT",)], w=[("PS", trb)])
        src = PSbf[trb][:, 0:nblk * 128].rearrange("p (k d) -> p k d", d=128)
        if eng == "act":
            act(dst3, src, AF.Copy, r=[("PS", trb)], w=[wtok])
        else:
            p.op("dve", lambda e: e.tensor_copy(dst3, src), r=[("PS", trb)], w=[wtok])

    def filters(g, zT, dec):
        n = g.n
        HTf = HTflat.bitcast(F32)
        ZT = HTf[0:33, 0:n]
        H1 = HTf[0:64, 2048:2048 + n]
        H2 = HTf[0:64, 4096:4096 + n]
        W3o = HTf[0:64, 6144:8192].rearrange("p (d c) -> p d c", c=D)
        W1 = HTf[0:33, 8192:8256]
        W2 = HTf[0:64, 8256:8320]
        FSM = HTf[0:64, 8320:8323]
        BROW = SQ[:, :, :].rearrange("p c t -> p (c t)").bitcast(F32)[0:1, 0:2048]
        p.dma("sp", ZT, zT[:, :], w=[("ZT",)])
        p.dma("sp", W1, f_w1[:, :], w=[("FW",)])
        p.dma("sp", W2, f_w2[:, :], w=[("FW",)])
        p.dma("sp", FSM, f_sm[:, :], w=[("FW",)])
        p.dma("sp", BROW, f_bias[:, :], w=[("BROW",)])
        TWO_PI = 2.0 * math.pi
        tw = min(512, n)
        for (W, src, dst, bcol, kdim, stok, dtok) in ((W1, ZT, H1, 0, 33, ("ZT",), ("H1",)), (W2, H1, H2, 1, 64, ("H1",), ("H2",))):
            for i in range(n // tw):
                bk = PJ[nxt("pj", 2)]
                mm(PS[bk][0:64, :tw], W, src[:, i * tw:(i + 1) * tw], True, True, r=[("FW",), stok], w=[("PS", bk)])
                k = nxt("tmp", 4)
                tsc(TMP[k][0:64, :tw], PS[bk][0:64, :tw], FSM[:, bcol:bcol + 1], FSM[:, 2:3], ALU.add, ALU.mult,
                    r=[("PS", bk), ("FW",)], w=[("TMP", k)])
                k2 = nxt("tmp", 4)
                MAGIC = 12582912.0
                tsc(TMP[k2][0:64, :tw], TMP[k][0:64, :tw], 1.0 / TWO_PI, MAGIC, ALU.mult, ALU.add, r=[("TMP", k)], w=[("TMP", k2)])
                tsc(TMP[k2][0:64, :tw], TMP[k2][0:64, :tw], -MAGIC, None, ALU.add, None, r=[("TMP", k2)], w=[("TMP", k2)])
                stt(TMP[k][0:64, :tw], TMP[k2][0:64, :tw], -TWO_PI, TMP[k][0:64, :tw], ALU.mult, ALU.add,
                    r=[("TMP", k), ("TMP", k2)], w=[("TMP", k)])
                tsc(TMP[k][0:64, :tw], TMP[k][0:64, :tw], 3.1415925, -3.1415925, ALU.min, ALU.max, r=[("TMP", k)], w=[("TMP", k)])
                act(dst[:, i * tw:(i + 1) * tw], TMP[k][0:64, :tw], AF.Sin, r=[("TMP", k)], w=[dtok])
        for o in range(2):
            p.dma("sp", W3o[:, 0, :], f_w3[:, o * D:(o + 1) * D], w=[("W3",)])
            p.dma("sp", W3o[:, 1, :], f_w3[:, 2 * D + o * D:2 * D + (o + 1) * D], w=[("W3",)])
            for tc in range(g.ntc):
                for half in range(2):
                    di = nxt("rs", 2)
                    p.dma("sp", RS[di][:, :], dec[tc * 128:(tc + 1) * 128, half * 512:(half + 1) * 512], w=[("RS", di)])
                    bF = SS4[nxt("ss", 4)]
                    bBk = SS4[nxt("ss", 4)]
                    mm(PS[bF][:, :], H2[:, tc * 128:(tc + 1) * 128], W3o[:, 0, half * 512:(half + 1) * 512], True, True,
                       r=[("H2",), ("W3",)], w=[("PS", bF)])
                    mm(PS[bBk][:, :], H2[:, tc * 128:(tc + 1) * 128], W3o[:, 1, half * 512:(half + 1) * 512], True, True,
                       r=[("H2",), ("W3",)], w=[("PS", bBk)])
                    k1 = nxt("tmp", 4)
                    k2 = nxt("tmp", 4)
                    tt(TMP[k1][:, :], PS[bF][:, :], RS[di][:, :], ALU.mult, r=[("PS", bF), ("RS", di)], w=[("TMP", k1)])
                    tt(TMP[k2][:, :], PS[bBk][:, :], RS[di][:, :], ALU.mult, r=[("PS", bBk), ("RS", di)], w=[("TMP", k2)])
                    if tc == 0:
                        tt(TMP[k1][0:1, :], TMP[k1][0:1, :], BROW[0:1, o * D + half * 512:o * D + (half + 1) * 512], ALU.add,
                           r=[("TMP", k1), ("BROW",)], w=[("TMP", k1)])
                    tt(Vtok[:, g.tc0 + tc, half * 512:(half + 1) * 512], TMP[k1][:, :], TMP[k2][:, :], ALU.add,
                       r=[("TMP", k1), ("TMP", k2)], w=[("KS",)])
                    tt(KDtok[:, g.tc0 + tc, half * 512:(half + 1) * 512], TMP[k1][:, :], TMP[k2][:, :], ALU.subtract,
                       r=[("TMP", k1), ("TMP", k2)], w=[("KD",)])
            for i in range(g.npair):
                FA, FB = load_F(g, i)
                for c in range(8):
                    bA, bB = fwd_banks(g, FA, FB, Vtok, KDtok, c, ("KS",), ("KD",))
                    k1 = nxt("tmp", 4)
                    k2 = nxt("tmp", 4)
                    act(TMP[k1][:, :g.tw], PS[bA][:, :g.tw], AF.Copy, r=[("PS", bA)], w=[("TMP", k1)])
                    p.op("dve", lambda e, k2=k2, bB=bB: e.tensor_copy(TMP[k2][:, :g.tw], PS[bB][:, :g.tw]), r=[("PS", bB)], w=[("TMP", k2)])
                    if i == 0:
                        p.op("dve", lambda e, k2=k2: e.memset(TMP[k2][:, 0:1], 0.0), r=[("TMP", k2)], w=[("TMP", k2)])
                        bN = PJ[nxt("pj", 2)]
                        for tc in range(g.ntc):
                            mm(PS[bN][:, 0:1], Vtok[:, g.tc0 + tc, c * 128:(c + 1) * 128], FB[:, tc, 0:1], tc == 0, tc == g.ntc - 1,
                               r=[("FB",), ("FC",), ("KS",)], w=[("PS", bN)])
                        p.op("dve", lambda e, bN=bN, o=o, c=c: e.tensor_copy(KN[:, g.gi, o, c:c + 1], PS[bN][:, 0:1]),
                             r=[("PS", bN)], w=[("KN",)])
                    p.dma("sp", g.KFD[o, 0, c, :, i * g.tw:(i + 1) * g.tw], TMP[k1][:, :g.tw], r=[("TMP", k1)], w=[("KFD",)])
                    p.dma("sp", g.KFD[o, 1, c, :, i * g.tw:(i + 1) * g.tw], TMP[k2][:, :g.tw], r=[("TMP", k2)], w=[("KFD",)])
            p.barrier(full=False)

    def hy_fwd(g, order):
        nblk = g.tw // 128
        pend = [None]
        for i in range(g.npair):
            FA, FB = load_F(g, i)
            for c in range(8):
                kp = nxt("kfb", 4)
                kq = nxt("kfb", 4)
                p.dma("sp", KFB[kp][:, :g.tw], g.KFD[order, 0, c, :, i * g.tw:(i + 1) * g.tw], w=[("KFB", kp)])
                p.dma("sp", KFB[kq][:, :g.tw], g.KFD[order, 1, c, :, i * g.tw:(i + 1) * g.tw], w=[("KFB", kq)])
                bA, bB = fwd_banks(g, FA, FB, Vtok, Vtok, c, ("VT", g.gi), ("VT", g.gi))
                yi = nxt("yt", 2)
                ks = [nxt("tmp", 4) for _ in range(4)]
                tw = g.tw
                tt(TMP[ks[0]][:, :tw], PS[bA][:, :tw], KFB[kp][:, :tw], ALU.mult, r=[("PS", bA), ("KFB", kp)], w=[("TMP", ks[0])])
                tt(TMP[ks[1]][:, :tw], PS[bB][:, :tw], KFB[kq][:, :tw], ALU.mult, r=[("PS", bB), ("KFB", kq)], w=[("TMP", ks[1])])
                tt(YT[yi][:, 0, :tw], TMP[ks[0]][:, :tw], TMP[ks[1]][:, :tw], ALU.subtract, r=[("TMP", ks[0]), ("TMP", ks[1])], w=[("YT", yi, 0)])
                tt(TMP[ks[2]][:, :tw], PS[bA][:, :tw], KFB[kq][:, :tw], ALU.mult, r=[("PS", bA), ("KFB", kq)], w=[("TMP", ks[2])])
                tt(TMP[ks[3]][:, :tw], PS[bB][:, :tw], KFB[kp][:, :tw], ALU.mult, r=[("PS", bB), ("KFB", kp)], w=[("TMP", ks[3])])
                tt(YT[yi][:, 1, :tw], TMP[ks[2]][:, :tw], TMP[ks[3]][:, :tw], ALU.add, r=[("TMP", ks[2]), ("TMP", ks[3])], w=[("YT", yi, 1)])
                if i == 0:
                    tsc(YT[yi][:, 1, 0:1], PS[bB][:, 0:1], KN[:, g.gi, order, c:c + 1], None, ALU.mult, None,
                        r=[("PS", bB), ("KN",), ("YT", yi, 1)], w=[("YT", yi, 1)])
                def trs(yi=yi, i=i, c=c):
                    for half, Y in ((0, g.YA), (1, g.YB)):
                        transposes_to(lambda blk, yi=yi, half=half: YT[yi][:, half, blk * 128:(blk + 1) * 128], nblk,
                                      Y[:, i * nblk:(i + 1) * nblk, c * 128:(c + 1) * 128], [("YT", yi, half)], ("YF", g.gi, half, c))
                if pend[0] is not None:
                    pend[0]()
                pend[0] = trs
        if pend[0] is not None:
            pend[0]()

    def hy_inv(g, conv):
        pend = [None]
        ttw = g.tw
        nblk = ttw // 128
        ntt = g.n // ttw
        for j in range(ntt):
            if g.gi == 0:
                p.dma("sp", XINb[0][:, :, :], g.GT[0:SEQ, j * 512:(j + 1) * 512].rearrange("(k p) t -> p k t", p=128), w=[("FA",)])
                p.dma("sp", XINb[1][:, :, :], g.GT[SEQ:2 * SEQ, j * 512:(j + 1) * 512].rearrange("(k p) t -> p k t", p=128), w=[("FB",)])
                GA, GBt = XINb[0], XINb[1]
            else:
                GA, GBt = GCT[:, 0:2, :], GCT[:, 2:4, :]
            tk0 = g.t0 + j * ttw
            ti = g.ti0 + j
            for c in range(8):
                ga = nxt("gb", 4)
                p.dma("sp", GB[ga][:, :ttw], GD[conv, c, :, tk0:tk0 + ttw], r=[("GD", conv, c)], w=[("GB", ga)])
                if conv == 1:
                    gs = nxt("gb", 4)
                    p.dma("sp", GB[gs][:, :ttw], GD[2, c, :, tk0:tk0 + ttw], r=[("GD", 2, c)], w=[("GB", gs)])
                bk = SS4[nxt("ss", 4)]
                for half, (Y, Gt, ftok) in enumerate(((g.YA, GA, ("FA",)), (g.YB, GBt, ("FB",)))):
                    for fc in range(g.nfh):
                        mm(PS[bk][:, :ttw], Y[:, fc, c * 128:(c + 1) * 128], Gt[:, fc, :ttw], half == 0 and fc == 0, half == 1 and fc == g.nfh - 1,
                           r=[("YF", g.gi, half, c), ftok, ("FC",)], w=[("PS", bk)])
                if conv == 0:
                    yi = nxt("yt", 2)
                    tt(YT[yi][:, 0, :ttw], PS[bk][:, :ttw], GB[ga][:, :ttw], ALU.mult, r=[("PS", bk), ("GB", ga)], w=[("YT", yi, 0)])
                    def trs(yi=yi, j=j, c=c):
                        transposes_to(lambda blk, yi=yi: YT[yi][:, 0, blk * 128:(blk + 1) * 128], nblk,
                                      Vtok[:, g.tc0 + j * nblk:g.tc0 + (j + 1) * nblk, c * 128:(c + 1) * 128], [("YT", yi, 0)], ("VT", g.gi))
                    if pend[0] is not None:
                        pend[0]()
                    pend[0] = trs
                else:
                    k = nxt("tmp", 4)
                    tt(TMP[k][:, :ttw], PS[bk][:, :ttw], GB[ga][:, :ttw], ALU.mult, r=[("PS", bk), ("GB", ga)], w=[("TMP", k)])
                    tt(BIG[:, c, tk0:tk0 + ttw], TMP[k][:, :ttw], GB[gs][:, :ttw], ALU.mult, r=[("TMP", k), ("GB", gs)], w=[("G", c, ti)])

        if pend[0] is not None:
            pend[0]()

    def hyena(l, b, with_ctx):
        geoms = [GX] + ([GC] if with_ctx else [])
        RAWS = [BIGhi[:, 0:2308], BIGhi[:, 6 * NT:6 * NT + 2308]]
        for ri in range(2):
            for col in (0, 257, 258, 2307):
                p.op("dve", lambda e, col=col, ri=ri: e.memset(RAWS[ri][:, col:col + 1], 0.0), w=[("RAW", ri)])
        def load_piece(pc_):
            w_ = nxt("wb", 2)
            load_w(WB[w_][:, :, 0:512], c_w_in[0][:, pc_ * 512:(pc_ + 1) * 512].rearrange("(c p) m -> p c m", p=128), ("WB", w_))
            return w_

        wi_next = load_piece(0)
        for pc in range(8):
            wi = wi_next
            if pc + 1 < 8:
                wi_next = load_piece(pc + 1)
            for mi in range(4):
                m = pc * 4 + mi
                kind, c = m // 8, m % 8
                oi = nxt("ost", 4)
                OST = BIG[:, 10 + oi, :]
                ri = m % 2
                RAW = RAWS[ri]
                for ti, (t0, n, is_ctx) in enumerate(TILES):
                    if is_ctx and not with_ctx:
                        continue
                    bk = PJ[nxt("pj", 2)]
                    for kc in range(8):
                        mm(PS[bk][:, :n], WB[wi][:, kc, mi * 128:(mi + 1) * 128], HT[:, kc, t0:t0 + n], kc == 0, kc == 7,
                           r=[("WB", wi), ("HT", kc, ti)], w=[("PS", bk)])
                    if kind == 3:
                        act(OST[:, t0:t0 + n], PS[bk][:, :n], AF.Silu, r=[("PS", bk)], w=[("OST", oi)])
                    else:
                        s0 = t0 + (1 if is_ctx else 3)
                        act(RAW[:, s0:s0 + n], PS[bk][:, :n], AF.Copy, r=[("PS", bk)], w=[("RAW", ri)])
                if kind < 3:
                    for ti, (t0, n, is_ctx) in enumerate(TILES):
                        if is_ctx and not with_ctx:
                            continue
                        s0 = t0 + (1 if is_ctx else 3)
                        k = nxt("tmp", 4)
                        tsc(TMP[k][:, :n], RAW[:, s0:s0 + n], CONVW[:, m, 1:2], CONVW[:, m, 3:4], ALU.mult, ALU.add,
                            r=[("RAW", ri), ("CONVW",)], w=[("TMP", k)], eng="pool")
                        stt(TMP[k][:, :n], RAW[:, s0 - 1:s0 - 1 + n], CONVW[:, m, 0:1], TMP[k][:, :n], ALU.mult, ALU.add,
                            r=[("RAW", ri), ("TMP", k), ("CONVW",)], w=[("TMP", k)])
                        stt(OST[:, t0:t0 + n], RAW[:, s0 + 1:s0 + 1 + n], CONVW[:, m, 2:3], TMP[k][:, :n], ALU.mult, ALU.add,
                            r=[("RAW", ri), ("TMP", k), ("CONVW",)], w=[("OST", oi)])
                if kind == 0:
                    for g in geoms:
                        for tcg in range(0, g.ntc, 4):
                            nb_ = min(4, g.ntc - tcg)
                            transposes_to(lambda blk, g=g, tcg=tcg, OST=OST: OST[:, (g.tc0 + tcg + blk) * 128:(g.tc0 + tcg + blk + 1) * 128], nb_,
                                          Vtok[:, g.tc0 + tcg:g.tc0 + tcg + nb_, c * 128:(c + 1) * 128], [("OST", oi)], ("VT", g.gi), eng="dve")
                else:
                    p.dma("sp", GD[kind - 1, c], OST[:, :], r=[("OST", oi)], w=[("GD", kind - 1, c)])
        p.barrier()
        for conv in range(2):
            for g in geoms:
                hy_fwd(g, conv)
            p.barrier()
            for g in geoms:
                hy_inv(g, conv)
            p.barrier()
        post(l, b, lambda kc, t0, n: BIG[:, kc, t0:t0 + n], lambda kc, ti: ("G", kc, ti), c_w_out[0], with_ctx, 8)

    if NL > 2:
        filters(GX, zT_x, dec_x)
        filters(GC, zT_c, dec_c)

    for b in range(NB):
        for l in range(NL):
            kind = l % 3
            with_ctx = l < DEPTH - 1
            if l == 0:
                prenorm(l, b)
            if kind == 0:
                mla(l, l // 3, b, with_ctx)
            elif kind == 1:
                swa(l, b, with_ctx)
            else:
                hyena(l, b, with_ctx)
            p.barrier(full=False)
    p.barrier()
    p.emit(es)
    es.close()
    return nc


def _rope_tables():
    rows = SEQ // 64
    row = np.repeat(np.arange(rows, dtype=np.float32), 64)
    col = np.tile(np.arange(64, dtype=np.float32), rows)
    per_axis = 32
    inv = (10000.0 ** (-np.arange(0, per_axis, 2, dtype=np.float32) / per_axis)).astype(np.float32)
    ang = np.concatenate([row[:, None] * inv, col[:, None] * inv], axis=-1).astype(np.float32)
    cos = np.cos(ang).astype(np.float32)
    sin = np.sin(ang).astype(np.float32)
    cosT = np.zeros((128, SEQ), np.float32)
    sinT = np.zeros((128, SEQ), np.float32)
    for pidx in range(128):
        d = pidx % 64
        cosT[pidx] = cos[:, d % 32]
        sinT[pidx] = -sin[:, d] if d < 32 else sin[:, d - 32]
    return cosT, sinT


def _colT(v, nch):
    L = v.shape[0]
    return np.ascontiguousarray(v.reshape(L, nch, 128).transpose(0, 2, 1))


PERM64 = np.concatenate([np.arange(32, 64), np.arange(0, 32)])


def prep_inputs(inp, NB, ncores):
    f = lambda a: np.ascontiguousarray(np.asarray(a, dtype=np.float32))
    x = f(inp["x"])
    ctx = f(inp["ctx"])
    c = f(inp["c"])
    c_ctx = f(inp["c_ctx"])
    cosT, sinT = _rope_tables()
    a_w_in = f(inp["a_w_in"])
    kr = a_w_in[:, :, 768:832]
    a_w_in2 = np.concatenate([a_w_in[:, :, 0:768], kr, kr[:, :, PERM64], a_w_in[:, :, 832:1856]], axis=2)
    a_w_q = f(inp["a_w_q"]).reshape(2, 512, 8, 192)
    a_w_q2 = np.concatenate([a_w_q, a_w_q[:, :, :, 128 + PERM64]], axis=3).reshape(2, 512, 8 * 256)
    b_w_in = f(inp["b_w_in"])[0]
    wA = []
    wB = []
    for hk in range(4):
        q = b_w_in[:, hk * 256:(hk + 1) * 256]
        qp = q.reshape(D, 4, 64)[:, :, PERM64].reshape(D, 256)
        k = b_w_in[:, 1024 + hk * 64:1024 + (hk + 1) * 64]
        kp = k[:, PERM64]
        v = b_w_in[:, 1280 + hk * 64:1280 + (hk + 1) * 64]
        g = b_w_in[:, 1536 + hk * 256:1536 + (hk + 1) * 256]
        wA.append(np.concatenate([q, qp], axis=1))
        wB.append(np.concatenate([k, k, kp, kp, v, g], axis=1))
    sink = f(inp["b_sink"])[0]
    sinkT = np.zeros((128, 8), np.float32)
    for pidx in range(128):
        for cc_ in range(8):
            sinkT[pidx, cc_] = sink[2 * cc_ + (1 if pidx >= 64 else 0)]
    ki = np.arange(128)[:, None]
    qi = np.arange(128)[None, :]
    maskT = np.concatenate([(ki <= qi), np.ones((128, 128), bool), (qi <= ki)], axis=1).astype(np.float32)
    def dft_mats(n):
        N2 = 2 * n
        t = np.arange(n, dtype=np.int64)[:, None]
        k = np.arange(n, dtype=np.int64)[None, :]
        ang = 2.0 * np.pi * ((t * k) % N2).astype(np.float64) / N2
        FT = np.zeros((n, N2), np.float64)
        FT[:, :n] = np.cos(ang)
        FT[:, n:] = -np.sin(ang)
        FT[:, n] = np.where(np.arange(n) % 2 == 0, 1.0, -1.0)
        GT = np.zeros((N2, n), np.float64)
        GT[:n, :] = (2.0 / N2) * np.cos(ang.T)
        GT[0, :] = 1.0 / N2
        GT[n:, :] = -(2.0 / N2) * np.sin(ang.T)
        GT[n, :] = np.where(np.arange(n) % 2 == 0, 1.0, -1.0) / N2
        return FT.astype(ml_dtypes.bfloat16), GT.astype(ml_dtypes.bfloat16)

    def pos_feats(n):
        t = np.linspace(0.0, 1.0, n, dtype=np.float32)[:, None]
        w = (np.float32(2.0 * math.pi / n) * np.arange(n, dtype=np.float32))[:, None]
        bands = np.linspace(1e-4, 15, 16, dtype=np.float32)[None, :]
        z = np.concatenate([t, np.cos(bands * w), -np.sin(bands * w)], axis=-1).astype(np.float32)
        cmin = math.log(1e-2) / 1.5
        cmax = math.log(1e-2) / 0.3
        deltas = np.abs(np.linspace(cmin, cmax, D, dtype=np.float32))
        dec = np.exp(-t * deltas[None, :]).astype(np.float32)
        return np.ascontiguousarray(z.T), np.ascontiguousarray(dec)

    FT_x, GT_x = dft_mats(SEQ)
    FT_c, GT_c = dft_mats(CTX)
    zT_x, dec_x = pos_feats(SEQ)
    zT_c, dec_c = pos_feats(CTX)
    cw = f(inp["c_conv_w"])[0]
    cb = f(inp["c_conv_b"])[0]
    convT = np.ascontiguousarray(np.concatenate([cw, cb[None, :]], axis=0).T.reshape(24, 128, 4).transpose(1, 0, 2))
    f_sm = np.ascontiguousarray(np.stack([f(inp["c_f_b1"])[0], f(inp["c_f_b2"])[0], f(inp["c_f_freq"])[0]], axis=1))
    shared = {
        "c_w_in": f(inp["c_w_in"]),
        "c_w_out": f(inp["c_w_out"]),
        "convT": convT,
        "f_w1": f(inp["c_f_w1"])[0],
        "f_w2": f(inp["c_f_w2"])[0],
        "f_w3": f(inp["c_f_w3"])[0],
        "f_sm": f_sm,
        "f_bias": np.ascontiguousarray(f(inp["c_filt_bias"])[0].reshape(1, 2048)),
        "zT_x": zT_x, "zT_c": zT_c, "dec_x": dec_x, "dec_c": dec_c,
        "FT_x": FT_x, "GT_x": GT_x, "FT_c": FT_c, "GT_c": GT_c,
        "identT": np.eye(128, dtype=np.float32),
        "b_wA": np.ascontiguousarray(np.stack(wA)),
        "b_wB": np.ascontiguousarray(np.stack(wB)),
        "b_w_out": f(inp["b_w_out"]),
        "sinkT": sinkT,
        "maskT": maskT,
        "w_mod": f(inp["w_mod"]),
        "b_modT": _colT(f(inp["b_mod"]), 24),
        "g_preT": _colT(f(inp["g_pre"]), 8),
        "g_postT": _colT(f(inp["g_post"]), 8),
        "a_w_in": np.ascontiguousarray(a_w_in2),
        "a_g_qT": _colT(f(inp["a_g_q"]), 4),
        "a_w_q": np.ascontiguousarray(a_w_q2),
        "a_g_kvT": _colT(f(inp["a_g_kv"]), 2),
        "a_w_kv": f(inp["a_w_kv"]),
        "a_w_out": f(inp["a_w_out"]),
        "cosT": cosT,
        "sinT": sinT,
    }
    maps = []
    for k in range(ncores):
        bs = slice(k * NB, (k + 1) * NB)
        xT = np.ascontiguousarray(x[bs].transpose(0, 2, 1)).reshape(NB, 8, 128, SEQ)
        cxT = np.ascontiguousarray(ctx[bs].transpose(0, 2, 1)).reshape(NB, 8, 128, CTX)
        cc = np.concatenate([c[bs], c_ctx[None, :]], axis=0)
        cT = np.ascontiguousarray(cc.T.reshape(8, 128, NB + 1).transpose(1, 0, 2))
        m = dict(shared)
        m.update({"xT": xT, "ctxT": cxT, "cT": cT})
        maps.append(m)
    return maps


def run(inp, NB, NL, ncores, trace=False):
    nc = build(NB, NL)
    maps = prep_inputs(inp, NB, ncores)
    res = run_bass_kernel_spmd(nc, maps, core_ids=list(range(ncores)), **({"trace": True} if trace else {}))
    outs = []
    for k in range(ncores):
        o = res.results[k]["outT"]
        outs.append(o.reshape(NB, D, SEQ).transpose(0, 2, 1))
    return np.ascontiguousarray(np.concatenate(outs, axis=0)), res


def kernel(**inputs):
    out, _ = run(inputs, BATCH // NCORES, DEPTH, NCORES)
    return out.astype(np.float32)
```

```python
import math
from contextlib import ExitStack
import numpy as np
import ml_dtypes
import concourse.bass as bass
import concourse.mybir as mybir
from concourse.bass_utils import run_bass_kernel_spmd

F32 = mybir.dt.float32
BF16 = mybir.dt.bfloat16
AF = mybir.ActivationFunctionType
ALU = mybir.AluOpType

D = 1024
SEQ = 2048
CTX = 256
NT = SEQ + CTX
BATCH = 32
DEPTH = 4
EPS = 1e-6
NCORES = 8


class Prog:
    def __init__(self, nc, ndma=24):
        self.nc = nc
        self.engs = ["pe", "act", "dve", "pool", "sp"]
        self.ops = {e: [] for e in self.engs}
        self.cnt = {e: 0 for e in self.engs}
        self.tw = {}
        self.tr = {}
        self.seen = {e: {} for e in self.engs}
        self.ndma = ndma
        self.dcnt = [0] * ndma
        self.drr = 0

    def _waits(self, eng, r, w, is_dma=False):
        raw = {}
        oth = {}
        for t in r:
            x = self.tw.get(t)
            if x:
                raw[x[0]] = max(raw.get(x[0], 0), x[1])
        for t in w:
            x = self.tw.get(t)
            if x:
                oth[x[0]] = max(oth.get(x[0], 0), x[1])
            for sk, v in self.tr.get(t, {}).items():
                oth[sk] = max(oth.get(sk, 0), v)
        res = {}
        for sk, v in raw.items():
            if sk == eng and eng == "pe":
                continue
            res[sk] = max(res.get(sk, 0), v)
        for sk, v in oth.items():
            if sk == eng and eng == "pe":
                continue
            res[sk] = max(res.get(sk, 0), v)
        out = []
        for sk, v in res.items():
            if self.seen[eng].get(sk, 0) >= v:
                continue
            self.seen[eng][sk] = v
            out.append((sk, v))
        return out

    def _done(self, key, val, r, w):
        for t in w:
            self.tw[t] = (key, val)
            self.tr[t] = {}
        for t in r:
            d = self.tr.setdefault(t, {})
            d[key] = max(d.get(key, 0), val)

    def op(self, eng, fn, r=(), w=()):
        for k, v in self._waits(eng, r, w):
            self.ops[eng].append(("w", k, v))
        self.cnt[eng] += 1
        self.ops[eng].append(("o", fn, eng, 1))
        self._done(eng, self.cnt[eng], r, w)

    def dma(self, q, out, in_, r=(), w=()):
        i = self.drr
        self.drr = (self.drr + 1) % self.ndma
        key = ("d", i)
        for k, v in self._waits(q, r, w, is_dma=True):
            self.ops[q].append(("w", k, v))
        prev = self.dcnt[i] * 16
        if prev and self.seen[q].get(key, 0) < prev:
            self.seen[q][key] = prev
            self.ops[q].append(("w", key, prev))
        self.dcnt[i] += 1
        self.ops[q].append(("o", (lambda e, o=out, i_=in_: e.dma_start(out=o, in_=i_)), key, 16))
        self._done(key, self.dcnt[i] * 16, r, w)

    def barrier(self, full=True):
        for e in self.engs:
            if not full and e == "pool":
                continue
            for k in ["pe", "act", "dve", "pool"]:
                if k != e and self.cnt[k] > self.seen[e].get(k, 0):
                    self.seen[e][k] = self.cnt[k]
                    self.ops[e].append(("w", k, self.cnt[k]))
            for i in range(self.ndma):
                v = self.dcnt[i] * 16
                if v > self.seen[e].get(("d", i), 0):
                    self.seen[e][("d", i)] = v
                    self.ops[e].append(("w", ("d", i), v))
        if full:
            self.tw = {}
            self.tr = {}

    def emit(self, es):
        nc = self.nc
        sems = {}
        for e in ["pe", "act", "dve", "pool"]:
            sems[e] = es.enter_context(nc.semaphore("s_" + e))
        for i in range(self.ndma):
            sems[("d", i)] = es.enter_context(nc.semaphore("sd%d" % i))
        block = es.enter_context(nc.Block())

        def run(name):
            def f(eng):
                for it in self.ops[name]:
                    if it[0] == "w":
                        eng.wait_ge(sems[it[1]], it[2])
                    else:
                        it[1](eng).then_inc(sems[it[2]], it[3])
            return f

        block.tensor(run("pe"))
        block.scalar(run("act"))
        block.vector(run("dve"))
        block.gpsimd(run("pool"))
        block.sync(run("sp"))


TILES = [(0, 256, True)] + [(256 + 512 * i, 512, False) for i in range(4)]


def build(NB, NL):
    nc = bass.Bass("TRN2", target_bir_lowering=False)
    es = ExitStack()
    p = Prog(nc)

    def din(name, shape, dt=F32):
        return nc.dram_tensor(name, list(shape), dt, kind="ExternalInput").ap()

    xT = din("xT", [NB, 8, 128, SEQ])
    ctxT = din("ctxT", [NB, 8, 128, CTX])
    cT = din("cT", [128, 8, NB + 1])
    w_mod = din("w_mod", [4, D, 3 * D])
    b_modT = din("b_modT", [4, 128, 24])
    g_preT = din("g_preT", [4, 128, 8])
    g_postT = din("g_postT", [4, 128, 8])
    a_w_in = din("a_w_in", [2, D, 1920])
    a_g_qT = din("a_g_qT", [2, 128, 4])
    a_w_q = din("a_w_q", [2, 512, 8 * 256])
    a_g_kvT = din("a_g_kvT", [2, 128, 2])
    a_w_kv = din("a_w_kv", [2, 256, 2048])
    a_w_out = din("a_w_out", [2, D, D])
    b_wA = din("b_wA", [4, D, 512])
    b_wB = din("b_wB", [4, D, 576])
    b_w_out = din("b_w_out", [1, D, D])
    sinkT = din("sinkT", [128, 8])
    maskT = din("maskT", [128, 384])
    c_w_in = din("c_w_in", [1, D, 4096])
    c_w_out = din("c_w_out", [1, D, D])
    convT = din("convT", [128, 24, 4])
    f_w1 = din("f_w1", [33, 64])
    f_w2 = din("f_w2", [64, 64])
    f_w3 = din("f_w3", [64, 4096])
    f_sm = din("f_sm", [64, 3])
    f_bias = din("f_bias", [1, 2048])
    zT_x = din("zT_x", [33, SEQ])
    zT_c = din("zT_c", [33, CTX])
    dec_x = din("dec_x", [SEQ, D])
    dec_c = din("dec_c", [CTX, D])
    FT_x = din("FT_x", [SEQ, 2 * SEQ], BF16)
    GT_x = din("GT_x", [2 * SEQ, SEQ], BF16)
    FT_c = din("FT_c", [CTX, 2 * CTX], BF16)
    GT_c = din("GT_c", [2 * CTX, CTX], BF16)
    identT = din("identT", [128, 128])
    KFD_x = nc.dram_tensor("kfd_x", [2, 2, 8, 128, SEQ], F32, kind="Internal").ap()
    KFD_c = nc.dram_tensor("kfd_c", [2, 2, 8, 128, CTX], F32, kind="Internal").ap()
    GD = nc.dram_tensor("gd_scr", [3, 8, 128, NT], BF16, kind="Internal").ap()
    cosT = din("cosT", [128, SEQ])
    sinT = din("sinT", [128, SEQ])
    outT = nc.dram_tensor("outT", [NB, 8, 128, SEQ], F32, kind="ExternalOutput").ap()
    RES = nc.dram_tensor("res_scr", [8, 128, NT], F32, kind="Internal").ap()

    def sb(name, shape, dt):
        return es.enter_context(nc.sbuf_tensor(name, list(shape), dt))

    HT = sb("HT", [128, 8, NT], BF16)
    BIG = sb("BIG", [128, 16, NT], BF16)
    XIN = [sb("XIN%d" % i, [128, 8, 512], F32) for i in range(2)]
    SQ = sb("SQ", [128, 8, 512], BF16)
    RS = [sb("RS%d" % i, [128, 512], F32) for i in range(2)]
    TMP = [sb("TMP%d" % i, [128, 512], F32) for i in range(4)]
    PT = [sb("PT%d" % i, [128, 512], BF16) for i in range(4)]
    DSUM = sb("DSUM", [128, 512], BF16)
    WB = [sb("WB%d" % i, [128, 8, 576], BF16) for i in range(2)]
    WQH = [sb("WQH%d" % i, [128, 4, 256], BF16) for i in range(2)]
    WKVH = [sb("WKVH%d" % i, [128, 2, 256], BF16) for i in range(2)]
    ONES = sb("ONES", [128, 128], BF16)
    ONLO = sb("ONLO", [128, 128], BF16)
    ONHI = sb("ONHI", [128, 128], BF16)
    MASK = sb("MASK", [128, 384], BF16)
    ES = sb("ES", [128, 8], F32)
    EPSC = sb("EPSC", [128, 1], F32)
    COS = sb("COS", [128, SEQ], BF16)
    SIN = sb("SIN", [128, SEQ], BF16)
    IDENT = sb("IDENT", [128, 128], BF16)
    FCA = sb("FCA", [128, 2, 256], BF16)
    FCB = sb("FCB", [128, 2, 256], BF16)
    GCT = sb("GCT", [128, 4, 256], BF16)
    KN = sb("KN", [128, 2, 2, 8], F32)
    CONVW = sb("CONVW", [128, 24, 4], F32)
    CT = sb("CT", [128, 8, NB + 1], F32)
    CS = sb("CS", [128, 8, NB + 1], BF16)
    MODR = sb("MODR", [128, 24, NB + 1], F32)
    MODS = sb("MODS", [128, 4, 3, 8, NB + 1], F32)
    BM = sb("BM", [128, 4, 24], F32)
    GPRE = sb("GPRE", [128, 4, 8], F32)
    GPOST = sb("GPOST", [128, 4, 8], F32)
    AGQ = sb("AGQ", [128, 2, 4], F32)
    AGKV = sb("AGKV", [128, 2, 2], F32)
    PS = [es.enter_context(nc.psum_tensor("PS%d" % i, [128, 512], F32)) for i in range(8)]
    S_B = [0, 1]
    O_B = [2, 3]
    D_B = [4, 5]
    PJ = [6, 7]
    rot = {"yb": 0, "pj": 0, "tmp": 0, "rs": 0, "wb": 0, "xin": 0, "pt": 0, "s": 0, "od": 0, "wh": 0}

    def nxt(k, n):
        v = rot[k]
        rot[k] = (v + 1) % n
        return v

    pj_state = {"banks": [6, 7], "i": 0}

    def set_pj(banks):
        pj_state["banks"] = list(banks)

    def pjb():
        pj_state["i"] = (pj_state["i"] + 1) % len(pj_state["banks"])
        return pj_state["banks"][pj_state["i"]]

    def mm(out, lhsT, rhs, start, stop, r, w):
        p.op("pe", lambda e: e.matmul(out, lhsT, rhs, start=start, stop=stop), r=r, w=w)

    def act(out, in_, func, r, w, bias=None, scale=None):
        kw = {}
        if bias is not None:
            kw["bias"] = bias
        if scale is not None:
            kw["scale"] = scale
        p.op("act", lambda e: e.activation(out=out, in_=in_, func=func, **kw), r=r, w=w)

    def tt(out, a, b, op, r, w, eng="dve"):
        p.op(eng, lambda e: e.tensor_tensor(out, a, b, op), r=r, w=w)

    def tsc(out, a, s1, s2, op0, op1, r, w, eng="dve"):
        if op1 is None:
            p.op(eng, lambda e: e.tensor_scalar(out, a, s1, None, op0), r=r, w=w)
        else:
            p.op(eng, lambda e: e.tensor_scalar(out, a, s1, s2, op0, op1), r=r, w=w)

    def stt(out, a, s, b, op0, op1, r, w, eng="dve"):
        p.op(eng, lambda e: e.scalar_tensor_tensor(out, a, s, b, op0, op1), r=r, w=w)

    def vcopy(out, in_, r, w):
        p.op("dve", lambda e: e.tensor_copy(out, in_), r=r, w=w)

    def rstd_from(psb, n, k_feats, rtoks):
        i = nxt("rs", 2)
        rs = RS[i]
        act(rs[:, :n], PS[psb][:, :n], AF.Ln, scale=1.0 / k_feats, bias=EPSC[:, 0:1], r=[("PS", psb), ("EPSC",)] + rtoks, w=[("RS", i)])
        act(rs[:, :n], rs[:, :n], AF.Exp, scale=-0.5, r=[("RS", i)], w=[("RS", i)])
        return rs, ("RS", i)

    def sumsq(src3, nch, n, rtoks):
        act(SQ[:, 0:nch, :n], src3, AF.Square, r=rtoks, w=[("SQ",)])
        b = pjb()
        for c in range(nch):
            mm(PS[b][:, :n], ONES[:, :], SQ[:, c, :n], c == 0, c == nch - 1, r=[("SQ",), ("ONES",)], w=[("PS", b)])
        return b

    def load_w(dst, src, wtok):
        p.dma("pool", dst, src, r=[("W16R",)], w=[wtok])

    W16 = {}

    def precast(name, src, lead, rows, cols):
        t16 = nc.dram_tensor(name + "_16", [lead, rows, cols], BF16, kind="Internal").ap()
        W16[name] = t16
        return [(t16[i, r0:r0 + 128, :], src[i, r0:r0 + 128, :]) for i in range(lead) for r0 in range(0, rows, 128)]

    p.op("dve", lambda e: e.memset(ONES[:, :], 1.0), w=[("ONES",)])
    p.op("dve", lambda e: e.memset(EPSC[:, :], EPS), w=[("EPSC",)])
    p.op("dve", lambda e: e.memset(ONLO[:, :], 0.0), w=[("ONES",)])
    p.op("dve", lambda e: e.memset(ONHI[:, :], 0.0), w=[("ONES",)])
    p.op("dve", lambda e: e.memset(ONLO[:, 0:64], 1.0), w=[("ONES",)])
    p.op("dve", lambda e: e.memset(ONHI[:, 64:128], 1.0), w=[("ONES",)])
    p.dma("pool", MASK[:, :], maskT[:, :], w=[("MASK",)])
    p.dma("sp", ES[:, :], sinkT[:, :], w=[("ES",)])
    act(ES[:, :], ES[:, :], AF.Exp, r=[("ES",)], w=[("ES",)])
    p.dma("pool", COS[:, :], cosT[:, :], w=[("COS",)])
    p.dma("pool", SIN[:, :], sinT[:, :], w=[("SIN",)])
    p.dma("pool", IDENT[:, :], identT[:, :], w=[("IDENT",)])
    p.dma("sp", FCA[:, :, :], FT_c[:, 0:256].rearrange("(k p) f -> p k f", p=128), w=[("FC",)])
    p.dma("sp", FCB[:, :, :], FT_c[:, 256:512].rearrange("(k p) f -> p k f", p=128), w=[("FC",)])
    p.dma("sp", GCT[:, :, :], GT_c.rearrange("(k p) t -> p k t", p=128), w=[("FC",)])
    p.dma("sp", CONVW[:, :, :], convT[:, :, :], w=[("CONVW",)])
    p.dma("sp", CT[:, :, :], cT[:, :, :], w=[("CT",)])
    p.dma("sp", BM[:, :, :], b_modT.rearrange("l p c -> p l c"), w=[("BM",)])
    p.dma("sp", GPRE[:, :, :], g_preT.rearrange("l p c -> p l c"), w=[("GP",)])
    p.dma("sp", GPOST[:, :, :], g_postT.rearrange("l p c -> p l c"), w=[("GP",)])
    p.dma("sp", AGQ[:, :, :], a_g_qT.rearrange("l p c -> p l c"), w=[("GP",)])
    p.dma("sp", AGKV[:, :, :], a_g_kvT.rearrange("l p c -> p l c"), w=[("GP",)])
    act(CS[:, :, :], CT[:, :, :], AF.Silu, r=[("CT",)], w=[("CS",)])
    NC1 = NB + 1
    for l in range(NL):
        b = pjb()
        for pc in range(6):
            wi = nxt("wb", 2)
            load_w(WB[wi][:, :, 0:512], w_mod[l, :, pc * 512:(pc + 1) * 512].rearrange("(c p) m -> p c m", p=128), ("WB", wi))
            for mi in range(4):
                m = pc * 4 + mi
                for kc in range(8):
                    mm(PS[b][:, m * NC1:(m + 1) * NC1], WB[wi][:, kc, mi * 128:(mi + 1) * 128], CS[:, kc, :],
                       kc == 0, kc == 7, r=[("WB", wi), ("CS",)], w=[("PS", b)])
        for jcol in range(NC1):
            tt(MODR[:, :, jcol], PS[b][:, 0:24 * NC1].rearrange("p (m j) -> p m j", j=NC1)[:, :, jcol], BM[:, l, :], ALU.add,
               r=[("PS", b), ("BM",)], w=[("MODR",)])
        tsc(MODR[:, 8:16, :], MODR[:, 8:16, :], 1.0, None, ALU.add, None, r=[("MODR",)], w=[("MODR",)])
        for jcol in range(NC1):
            tt(MODS[:, l, 0, :, jcol], MODR[:, 8:16, jcol], GPRE[:, l, :], ALU.mult, r=[("MODR",), ("GP",)], w=[("MODS",)])
            tt(MODS[:, l, 2, :, jcol], MODR[:, 16:24, jcol], GPOST[:, l, :], ALU.mult, r=[("MODR",), ("GP",)], w=[("MODS",)])
        p.op("dve", lambda e, l=l: e.tensor_copy(MODS[:, l, 1, :, :], MODR[:, 0:8, :]), r=[("MODR",)], w=[("MODS",)])
    jobs = []
    jobs += precast("a_w_in", a_w_in, 2, D, 1920)
    jobs += precast("a_w_q", a_w_q, 2, 512, 2048)
    jobs += precast("a_w_kv", a_w_kv, 2, 256, 2048)
    jobs += precast("a_w_out", a_w_out, 2, D, D)
    jobs += precast("b_wA", b_wA, 4, D, 512)
    jobs += precast("b_wB", b_wB, 4, D, 576)
    jobs += precast("b_w_out", b_w_out, 1, D, D)
    jobs += precast("c_w_in", c_w_in, 1, D, 4096)
    jobs += precast("c_w_out", c_w_out, 1, D, D)
    n_first = 2 * 8 + 2 * 4 + 2 * 2 + 2 * 8
    for (dst_, src_) in jobs[:n_first]:
        p.dma("pool", dst_, src_, w=[("W16R",)])
    a_w_in, a_w_q, a_w_kv, a_w_out = W16["a_w_in"], W16["a_w_q"], W16["a_w_kv"], W16["a_w_out"]
    b_wA, b_wB, b_w_out = W16["b_wA"], W16["b_wB"], W16["b_w_out"]
    c_w_in, c_w_out = W16["c_w_in"], W16["c_w_out"]
    p.barrier(full=False)
    for (dst_, src_) in jobs[n_first:]:
        p.dma("pool", dst_, src_, w=[("W16R",)])

    def res_src(l, b, t0, n, is_ctx):
        if l == 0:
            if is_ctx:
                return ctxT[b].rearrange("c p t -> p c t")[:, :, t0:t0 + n]
            return xT[b].rearrange("c p t -> p c t")[:, :, t0 - CTX:t0 - CTX + n]
        return RES.rearrange("c p t -> p c t")[:, :, t0:t0 + n]

    def norm_stats(xi, n):
        bk = sumsq(XIN[xi][:, :, :n], 8, n, [("XIN", xi)])
        return rstd_from(bk, n, D, [])

    def modulate(l, b, ti, xi, rs, rtok, chunks=range(8)):
        t0, n, is_ctx = TILES[ti]
        col = NB if is_ctx else b
        for c in chunks:
            k = nxt("tmp", 4)
            stt(TMP[k][:, :n], XIN[xi][:, c, :n], MODS[:, l, 0, c, col:col + 1], rs[:, :n], ALU.mult, ALU.mult,
                r=[("XIN", xi), rtok, ("MODS",)], w=[("TMP", k)])
            if c < 5:
                act(HT[:, c, t0:t0 + n], TMP[k][:, :n], AF.Identity, bias=MODS[:, l, 1, c, col:col + 1],
                    r=[("TMP", k), ("MODS",)], w=[("HT", c, ti)])
            else:
                tsc(HT[:, c, t0:t0 + n], TMP[k][:, :n], MODS[:, l, 1, c, col:col + 1], None, ALU.add, None,
                    r=[("TMP", k), ("MODS",)], w=[("HT", c, ti)], eng="pool")

    def prenorm(l, b):
        def stats(ti):
            t0, n, is_ctx = TILES[ti]
            xi = nxt("xin", 2)
            p.dma("sp", XIN[xi][:, :, :n], res_src(l, b, t0, n, is_ctx), r=[("RES", ti)], w=[("XIN", xi)])
            rs, rtok = norm_stats(xi, n)
            return xi, rs, rtok

        cur = stats(0)
        for ti in range(len(TILES)):
            nx = stats(ti + 1) if ti + 1 < len(TILES) else None
            modulate(l, b, ti, *cur)
            cur = nx

    def post(l, b, gsrc, gtok, w_out_ap, with_ctx, r0):
        last = (l == NL - 1)
        region = BIG[:, r0:r0 + 8, :].rearrange("p c t -> p (c t)")
        WO = region[:, 0:8 * D].rearrange("p (c m) -> p c m", m=D)
        YBS = [region[:, 8192 + y * 4096:8192 + (y + 1) * 4096].rearrange("p (c t) -> p c t", t=512) for y in range(2)]
        for half in range(2):
            load_w(WO[:, :, half * 512:(half + 1) * 512], w_out_ap[:, half * 512:(half + 1) * 512].rearrange("(c p) m -> p c m", p=128), ("WO", half))
        tiles = [ti for ti, (t0, n, is_ctx) in enumerate(TILES) if not (is_ctx and not with_ctx)]
        SB2 = [2, 3]
        set_pj([6, 7, 0, 1, 4, 5])

        def stream_A(ti):
            t0, n, is_ctx = TILES[ti]
            xi = nxt("xin", 2)
            ybi = nxt("yb", 2)
            YB = YBS[ybi]
            p.dma("sp", XIN[xi][:, :, :n], res_src(l, b, t0, n, is_ctx), r=[("RES", ti)], w=[("XIN", xi)])

            def grp(m):
                def f():
                    bk = pjb()
                    for kc in range(8):
                        mm(PS[bk][:, :n], WO[:, kc, m * 128:(m + 1) * 128], gsrc(kc, t0, n), kc == 0, kc == 7,
                           r=[("WO", m // 4), gtok(kc, ti)], w=[("PS", bk)])
                    act(YB[:, m, :n], PS[bk][:, :n], AF.Copy, r=[("PS", bk)], w=[("YB", ybi, m)])
                    act(SQ[:, m, :n], PS[bk][:, :n], AF.Square, r=[("PS", bk)], w=[("SQ",)])
                return f
            return (xi, ybi), [grp(m) for m in range(8)]

        def stream_B(ti, xi, ybi):
            t0, n, is_ctx = TILES[ti]
            col = NB if is_ctx else b
            YB = YBS[ybi]
            st = {}

            def g0():
                bk = SB2[0]
                for c in range(8):
                    mm(PS[bk][:, :n], ONES[:, :], SQ[:, c, :n], c == 0, c == 7, r=[("SQ",), ("ONES",)], w=[("PS", bk)])
                st["rs"] = rstd_from(bk, n, D, [])

            def gm(m0):
                def f():
                    rs, rtok = st["rs"]
                    for m in (m0, m0 + 1):
                        k = nxt("tmp", 4)
                        tt(TMP[k][:, :n], YB[:, m, :n], rs[:, :n], ALU.mult, r=[("YB", ybi, m), rtok], w=[("TMP", k)])
                        stt(XIN[xi][:, m, :n], TMP[k][:, :n], MODS[:, l, 2, m, col:col + 1], XIN[xi][:, m, :n], ALU.mult, ALU.add,
                            r=[("TMP", k), ("XIN", xi), ("MODS",)], w=[("XIN", xi)])
                return f

            def g5():
                if last:
                    if not is_ctx:
                        p.dma("sp", outT[b].rearrange("c p t -> p c t")[:, :, t0 - CTX:t0 - CTX + n], XIN[xi][:, :, :n],
                              r=[("XIN", xi)], w=[("OUT", ti)])
                    return
                p.dma("sp", RES.rearrange("c p t -> p c t")[:, :, t0:t0 + n], XIN[xi][:, :, :n], r=[("XIN", xi)], w=[("RES", ti)])
                act(YB[:, :, :n], XIN[xi][:, :, :n], AF.Square, r=[("XIN", xi)], w=[("YB", ybi, m) for m in range(8)])
                bk = SB2[1]
                for c in range(8):
                    mm(PS[bk][:, :n], ONES[:, :], YB[:, c, :n], c == 0, c == 7, r=[("YB", ybi, c), ("ONES",)], w=[("PS", bk)])
                st["rs2"] = rstd_from(bk, n, D, [])

            def g67(chunks):
                def f():
                    if not last:
                        rs2, rtok2 = st["rs2"]
                        modulate(l + 1, b, ti, xi, rs2, rtok2, chunks)
                return f
            return [g0, gm(0), gm(2), gm(4), gm(6), g5, g67(range(0, 4)), g67(range(4, 8))]

        prevB = None
        for ti in tiles:
            info, thA = stream_A(ti)
            if prevB is None:
                for fa in thA:
                    fa()
            else:
                for fa, fb in zip(thA, prevB):
                    fb()
                    fa()
            prevB = stream_B(ti, *info)
        for fb in prevB:
            fb()

    def copy_through(l, b):
        pass

    def rope_combine(dst, bA, bB, t0, n, rows, rtoks, wtok):
        s0 = t0 - CTX
        k1 = nxt("tmp", 4)
        k2 = nxt("tmp", 4)
        tt(TMP[k1][rows, :n], PS[bA][rows, :n], COS[rows, s0:s0 + n], ALU.mult, r=[("PS", bA), ("COS",)] + rtoks, w=[("TMP", k1)])
        tt(TMP[k2][rows, :n], PS[bB][rows, :n], SIN[rows, s0:s0 + n], ALU.mult, r=[("PS", bB), ("SIN",)] + rtoks, w=[("TMP", k2)])
        tt(dst, TMP[k1][rows, :n], TMP[k2][rows, :n], ALU.add, r=[("TMP", k1), ("TMP", k2)], w=[wtok])

    def mla(l, j, b, with_ctx):
        scale = (128 + 64) ** -0.5
        win = a_w_in[j]
        set_pj([6, 7, 0, 1, 2, 3, 4, 5])
        for pc in range(4):
            ncols = 512 if pc < 3 else 384
            wi = nxt("wb", 2)
            load_w(WB[wi][:, :, :ncols], win[:, pc * 512:pc * 512 + ncols].rearrange("(c p) m -> p c m", p=128), ("WB", wi))
            for mi in range(ncols // 128):
                m = pc * 4 + mi
                for ti, (t0, n, is_ctx) in enumerate(TILES):
                    if m == 6:
                        bA = pjb()
                        bB = pjb()
                        for kc in range(8):
                            mm(PS[bA][0:64, :n], WB[wi][:, kc, mi * 128:mi * 128 + 64], HT[:, kc, t0:t0 + n], kc == 0, kc == 7,
                               r=[("WB", wi), ("HT", kc, ti)], w=[("PS", bA)])
                        if is_ctx:
                            act(BIG[0:64, 6, t0:t0 + n], PS[bA][0:64, :n], AF.Copy, r=[("PS", bA)], w=[("KR", ti)])
                            continue
                        for kc in range(8):
                            mm(PS[bB][0:64, :n], WB[wi][:, kc, mi * 128 + 64:mi * 128 + 128], HT[:, kc, t0:t0 + n], kc == 0, kc == 7,
                               r=[("WB", wi), ("HT", kc, ti)], w=[("PS", bB)])
                        rope_combine(BIG[0:64, 6, t0:t0 + n], bA, bB, t0, n, slice(0, 64), [], ("KR", ti))
                        continue
                    bk = pjb()
                    for kc in range(8):
                        mm(PS[bk][:, :n], WB[wi][:, kc, mi * 128:(mi + 1) * 128], HT[:, kc, t0:t0 + n], kc == 0, kc == 7,
                           r=[("WB", wi), ("HT", kc, ti)], w=[("PS", bk)])
                    if m < 6:
                        act(BIG[:, m, t0:t0 + n], PS[bk][:, :n], AF.Copy, r=[("PS", bk)], w=[("CQ", m, ti)])
                    else:
                        act(BIG[:, m + 1, t0:t0 + n], PS[bk][:, :n], AF.Silu, r=[("PS", bk)], w=[("G", m - 7, ti)])
        for ti, (t0, n, is_ctx) in enumerate(TILES):
            for (c0, nch, gt, kf) in ((0, 4, AGQ, 512), (4, 2, AGKV, 256)):
                if c0 == 0 and is_ctx and not with_ctx:
                    continue
                bk = sumsq(BIG[:, c0:c0 + nch, t0:t0 + n], nch, n, [("CQ", c0 + i, ti) for i in range(nch)])
                rs, rtok = rstd_from(bk, n, kf, [])
                for c in range(nch):
                    stt(BIG[:, c0 + c, t0:t0 + n], BIG[:, c0 + c, t0:t0 + n], gt[:, j, c:c + 1], rs[:, :n], ALU.mult, ALU.mult,
                        r=[("CQ", c0 + c, ti), rtok, ("GP",)], w=[("CQ", c0 + c, ti)])
        p.barrier(full=False)
        p.op("dve", lambda e: e.memset(HT[64:128, 3, :], 0.0), w=[("QR", ti_) for ti_ in range(5)])
        p.op("dve", lambda e: e.memset(BIG[64:128, 6, :], 0.0), w=[("KR", ti_) for ti_ in range(5)])
        QN = HT[:, 0, :]
        KN = HT[:, 1, :]
        V = HT[:, 2, :].rearrange("p (k d) -> p k d", d=128)
        QR = HT[:, 3, :]
        KR = BIG[:, 6, :]
        for h in range(8):
            wh = nxt("wh", 2)
            load_w(WQH[wh][:, :, :], a_w_q[j, :, h * 256:(h + 1) * 256].rearrange("(c p) m -> p c m", p=128), ("WQH", wh))
            load_w(WKVH[wh][:, :, :], a_w_kv[j, :, h * 256:(h + 1) * 256].rearrange("(c p) m -> p c m", p=128), ("WKVH", wh))
            for ti, (t0, n, is_ctx) in enumerate(TILES):
                bk = pjb()
                for kc in range(2):
                    mm(PS[bk][:, :n], WKVH[wh][:, kc, 0:128], BIG[:, 4 + kc, t0:t0 + n], kc == 0, kc == 1,
                       r=[("WKVH", wh), ("CQ", 4 + kc, ti)], w=[("PS", bk)])
                act(KN[:, t0:t0 + n], PS[bk][:, :n], AF.Copy, r=[("PS", bk)], w=[("KN", ti)])
                bk = pjb()
                nk = n // 128
                for kk in range(nk):
                    for kc in range(2):
                        mm(PS[bk][:, kk * 128:(kk + 1) * 128], BIG[:, 4 + kc, t0 + kk * 128:t0 + (kk + 1) * 128], WKVH[wh][:, kc, 128:256],
                           kc == 0, kc == 1, r=[("WKVH", wh), ("CQ", 4 + kc, ti)], w=[("PS", bk)])
                p.op("dve", lambda e, bk=bk, t0=t0, nk=nk, n=n: e.tensor_copy(
                    V[:, t0 // 128:t0 // 128 + nk, :], PS[bk][:, :n].rearrange("p (k d) -> p k d", d=128)),
                    r=[("PS", bk)], w=[("V", ti)])
                if is_ctx and not with_ctx:
                    continue
                bk = pjb()
                for kc in range(4):
                    mm(PS[bk][:, :n], WQH[wh][:, kc, 0:128], BIG[:, kc, t0:t0 + n], kc == 0, kc == 3,
                       r=[("WQH", wh), ("CQ", kc, ti)], w=[("PS", bk)])
                act(QN[:, t0:t0 + n], PS[bk][:, :n], AF.Copy, r=[("PS", bk)], w=[("QN", ti)])
                bA = pjb()
                for kc in range(4):
                    mm(PS[bA][0:64, :n], WQH[wh][:, kc, 128:192], BIG[:, kc, t0:t0 + n], kc == 0, kc == 3,
                       r=[("WQH", wh), ("CQ", kc, ti)], w=[("PS", bA)])
                if is_ctx:
                    act(QR[0:64, t0:t0 + n], PS[bA][0:64, :n], AF.Copy, r=[("PS", bA)], w=[("QR", ti)])
                else:
                    bB = pjb()
                    for kc in range(4):
                        mm(PS[bB][0:64, :n], WQH[wh][:, kc, 192:256], BIG[:, kc, t0:t0 + n], kc == 0, kc == 3,
                           r=[("WQH", wh), ("CQ", kc, ti)], w=[("PS", bB)])
                    rope_combine(QR[0:64, t0:t0 + n], bA, bB, t0, n, slice(0, 64), [], ("QR", ti))
            for ti, (t0, n, is_ctx) in enumerate(TILES):
                if is_ctx and not with_ctx:
                    continue
                nkc = 2 if is_ctx else 18
                od = nxt("od", 2)
                ob, db = O_B[od], D_B[od]
                def st1(kc):
                    kti = 0 if kc < 2 else 1 + (kc - 2) // 4
                    sbk = (0, 1, 6, 7)[nxt("s", 4)]
                    mm(PS[sbk][:, :n], KN[:, kc * 128:(kc + 1) * 128], QN[:, t0:t0 + n], True, False,
                       r=[("KN", kti), ("QN", ti)], w=[("PS", sbk)])
                    mm(PS[sbk][:, :n], KR[:, kc * 128:(kc + 1) * 128], QR[:, t0:t0 + n], False, True,
                       r=[("KR", kti), ("QR", ti)], w=[("PS", sbk)])
                    pi = nxt("pt", 4)
                    act(PT[pi][:, :n], PS[sbk][:, :n], AF.Exp, scale=scale, r=[("PS", sbk)], w=[("PT", pi)])
                    return (kc, kti, pi)

                def st2(kc, kti, pi):
                    mm(PS[ob][:, :n], V[:, kc, :], PT[pi][:, :n], kc == 0, kc == nkc - 1, r=[("V", kti), ("PT", pi)], w=[("PS", ob)])
                    if kc == 0:
                        vcopy(PS[db][:, :n], PT[pi][:, :n], r=[("PT", pi)], w=[("PS", db)])
                    elif kc == 1:
                        p.op("pool", lambda e, o_=RS[od][:, :n], i_=PT[pi][:, :n]: e.tensor_copy(o_, i_), r=[("PT", pi)], w=[("RS", od)])
                    elif kc % 2 == 0:
                        tt(PS[db][:, :n], PS[db][:, :n], PT[pi][:, :n], ALU.add, r=[("PS", db), ("PT", pi)], w=[("PS", db)])
                    else:
                        tt(RS[od][:, :n], RS[od][:, :n], PT[pi][:, :n], ALU.add, r=[("RS", od), ("PT", pi)], w=[("RS", od)], eng="pool")

                pend = []
                for kc in range(nkc):
                    pend.append(st1(kc))
                    if len(pend) > 2:
                        st2(*pend.pop(0))
                while pend:
                    st2(*pend.pop(0))
                tt(DSUM[:, :n], PS[db][:, :n], RS[od][:, :n], ALU.add, r=[("PS", db), ("RS", od)], w=[("DSUM",)])
                mm(PS[db][:, :n], ONES[:, :], DSUM[:, :n], True, True, r=[("ONES",), ("DSUM",)], w=[("PS", db)])
                k1 = nxt("tmp", 4)
                k2 = nxt("tmp", 4)
                act(TMP[k1][:, :n], PS[db][:, :n], AF.Ln, r=[("PS", db)], w=[("TMP", k1)])
                act(TMP[k1][:, :n], TMP[k1][:, :n], AF.Exp, scale=-1.0, r=[("TMP", k1)], w=[("TMP", k1)])
                tt(TMP[k2][:, :n], PS[ob][:, :n], TMP[k1][:, :n], ALU.mult, r=[("PS", ob), ("TMP", k1)], w=[("TMP", k2)])
                tt(BIG[:, 8 + h, t0:t0 + n], TMP[k2][:, :n], BIG[:, 8 + h, t0:t0 + n], ALU.mult, r=[("TMP", k2), ("G", h, ti)], w=[("G", h, ti)])
        p.barrier()
        post(l, b, lambda kc, t0, n: BIG[:, 8 + kc, t0:t0 + n], lambda kc, ti: ("G", kc, ti), a_w_out[j], with_ctx, 0)

    def swa(l, b, with_ctx):
        scale = 64 ** -0.5
        SS = [0, 1, 2, 3]
        Kt = BIG[:, 2, :]
        VLO = BIG[:, 3, :].rearrange("p (k d) -> p k d", d=128)
        VHI = BIG[:, 4, :].rearrange("p (k d) -> p k d", d=128)
        allv = [("VV", ti) for ti in range(5)]
        p.op("dve", lambda e: e.memset(VLO[:, :, 64:128], 0.0), w=allv)
        p.op("dve", lambda e: e.memset(VHI[:, :, 0:64], 0.0), w=allv)
        for hk in range(4):
            set_pj([6, 7, 0, 1, 2, 3, 4, 5])
            wa = nxt("wb", 2)
            load_w(WB[wa][:, :, 0:512], b_wA[hk].rearrange("(c p) m -> p c m", p=128), ("WB", wa))
            wbi = nxt("wb", 2)
            load_w(WB[wbi][:, :, 0:576], b_wB[hk].rearrange("(c p) m -> p c m", p=128), ("WB", wbi))
            for ti, (t0, n, is_ctx) in enumerate(TILES):
                jobs = [(("Q", qc, ti), BIG[:, qc, t0:t0 + n], wa, qc * 128, 256 + qc * 128) for qc in range(2)]
                jobs.append((("K", ti), Kt[:, t0:t0 + n], wbi, 0, 128))
                for (tok, dst, wsel, oA, oB) in jobs:
                    if tok[0] == "Q" and is_ctx and not with_ctx:
                        continue
                    bA = pjb()
                    for kc in range(8):
                        mm(PS[bA][:, :n], WB[wsel][:, kc, oA:oA + 128], HT[:, kc, t0:t0 + n], kc == 0, kc == 7,
                           r=[("WB", wsel), ("HT", kc, ti)], w=[("PS", bA)])
                    if is_ctx:
                        act(dst, PS[bA][:, :n], AF.Copy, r=[("PS", bA)], w=[tok])
                    else:
                        bB = pjb()
                        for kc in range(8):
                            mm(PS[bB][:, :n], WB[wsel][:, kc, oB:oB + 128], HT[:, kc, t0:t0 + n], kc == 0, kc == 7,
                               r=[("WB", wsel), ("HT", kc, ti)], w=[("PS", bB)])
                        rope_combine(dst, bA, bB, t0, n, slice(0, 128), [], tok)
                bk = pjb()
                nk = n // 128
                for kk in range(nk):
                    for kc in range(8):
                        mm(PS[bk][:, kk * 64:(kk + 1) * 64], HT[:, kc, t0 + kk * 128:t0 + (kk + 1) * 128], WB[wbi][:, kc, 256:320],
                           kc == 0, kc == 7, r=[("WB", wbi), ("HT", kc, ti)], w=[("PS", bk)])
                k0 = t0 // 128
                p.op("dve", lambda e, bk=bk, k0=k0, nk=nk: e.tensor_copy(
                    VLO[:, k0:k0 + nk, 0:64], PS[bk][:, :nk * 64].rearrange("p (k d) -> p k d", d=64)),
                    r=[("PS", bk)], w=[("VV", ti)])
                p.op("dve", lambda e, bk=bk, k0=k0, nk=nk: e.tensor_copy(
                    VHI[:, k0:k0 + nk, 64:128], PS[bk][:, :nk * 64].rearrange("p (k d) -> p k d", d=64)),
                    r=[("PS", bk)], w=[("VV", ti)])
            set_pj([6, 7])
            for qc in range(2):
                c = hk * 2 + qc
                for ti, (t0, n, is_ctx) in enumerate(TILES):
                    if is_ctx and not with_ctx:
                        continue
                    ob, db = 4, 5
                    items = [(0, 0, n, None), (1, 0, n, None)]
                    if not is_ctx:
                        i0 = 4 * (ti - 1)
                        for jb in range(i0 - 1, i0 + 5):
                            if jb < 0 or jb >= 16:
                                continue
                            qlo = max(i0, jb - 1)
                            qhi = min(i0 + 3, jb + 1)
                            items.append((2 + jb, (qlo - i0) * 128, (qhi - i0 + 1) * 128, (qlo - jb + 1) * 128))
                    gb = pjb()
                    for kc in range(8):
                        mm(PS[gb][:, :n], WB[wbi][:, kc, 320 + qc * 128:320 + (qc + 1) * 128], HT[:, kc, t0:t0 + n], kc == 0, kc == 7,
                           r=[("WB", wbi), ("HT", kc, ti)], w=[("PS", gb)])
                    kg = nxt("tmp", 4)
                    act(TMP[kg][:, :n], PS[gb][:, :n], AF.Silu, r=[("PS", gb)], w=[("TMP", kg)])
                    def s1(ii, kc, c0, c1, m0):
                        kti = 0 if kc < 2 else 1 + (kc - 2) // 4
                        nn = c1 - c0
                        pts = []
                        for half, rows in ((0, slice(0, 64)), (1, slice(64, 128))):
                            sbk = SS[nxt("s", 4)]
                            mm(PS[sbk][:, :nn], Kt[rows, kc * 128:(kc + 1) * 128], BIG[rows, qc, t0 + c0:t0 + c1], True, True,
                               r=[("K", kti), ("Q", qc, ti)], w=[("PS", sbk)])
                            pi = nxt("pt", 4)
                            act(PT[pi][:, :nn], PS[sbk][:, :nn], AF.Exp, scale=scale, r=[("PS", sbk)], w=[("PT", pi)])
                            if m0 is not None:
                                tt(PT[pi][:, :nn], PT[pi][:, :nn], MASK[:, m0:m0 + nn], ALU.mult, r=[("PT", pi), ("MASK",)], w=[("PT", pi)])
                            pts.append(pi)
                        return (ii, kc, kti, c0, c1, nn, pts)

                    def s2(ii, kc, kti, c0, c1, nn, pts):
                        first = ii == 0
                        lastm = ii == len(items) - 1
                        for half, (vv, oo) in enumerate(((VLO, ONLO), (VHI, ONHI))):
                            pi = pts[half]
                            mm(PS[ob][:, c0:c1], vv[:, kc, :], PT[pi][:, :nn], first and half == 0, lastm and half == 1,
                               r=[("VV", kti), ("PT", pi)], w=[("PS", ob)])
                            mm(PS[db][:, c0:c1], oo[:, :], PT[pi][:, :nn], first and half == 0, lastm and half == 1,
                               r=[("ONES",), ("PT", pi)], w=[("PS", db)])

                    prev = None
                    for ii, it in enumerate(items):
                        cur = s1(ii, *it)
                        if prev is not None:
                            s2(*prev)
                        prev = cur
                    s2(*prev)
                    k1 = nxt("tmp", 4)
                    k2 = nxt("tmp", 4)
                    tsc(TMP[k1][:, :n], PS[db][:, :n], ES[:, c:c + 1], None, ALU.add, None, r=[("PS", db), ("ES",)], w=[("TMP", k1)])
                    act(TMP[k1][:, :n], TMP[k1][:, :n], AF.Ln, r=[("TMP", k1)], w=[("TMP", k1)])
                    act(TMP[k1][:, :n], TMP[k1][:, :n], AF.Exp, scale=-1.0, r=[("TMP", k1)], w=[("TMP", k1)])
                    tt(TMP[k2][:, :n], PS[ob][:, :n], TMP[k1][:, :n], ALU.mult, r=[("PS", ob), ("TMP", k1)], w=[("TMP", k2)])
                    tt(BIG[:, 8 + c, t0:t0 + n], TMP[k2][:, :n], TMP[kg][:, :n], ALU.mult, r=[("TMP", k2), ("TMP", kg)], w=[("G", c, ti)])
        p.barrier()
        post(l, b, lambda kc, t0, n: BIG[:, 8 + kc, t0:t0 + n], lambda kc, ti: ("G", kc, ti), b_w_out[0], with_ctx, 0)

    HTflat = HT[:, :, :].rearrange("p c t -> p (c t)")
    BIGlo = BIG[:, 0:8, :].rearrange("p c t -> p (c t)")
    BIGhi = BIG[:, 8:16, :].rearrange("p c t -> p (c t)")
    Vtok = BIGlo.rearrange("p (k d) -> p k d", d=D)
    KDtok = BIGhi.rearrange("p (k d) -> p k d", d=D)
    XINb = [XIN[i][:, :, :].rearrange("p c t -> p (c t)").bitcast(BF16).rearrange("p (k f) -> p k f", f=512) for i in range(2)]
    KFB = [WB[0][:, :, :].rearrange("p c t -> p (c t)").bitcast(F32)[:, k * 512:(k + 1) * 512] for k in range(4)]
    WB1f = WB[1][:, :, :].rearrange("p c t -> p (c t)")
    GB = [WB1f[:, k * 512:(k + 1) * 512] for k in range(4)]
    YT = [WB1f[:, 2048 + y * 1024:2048 + (y + 1) * 1024].rearrange("p (h f) -> p h f", f=512) for y in range(2)]
    PSbf = [PS[i][:, :].bitcast(BF16) for i in range(8)]
    SS4 = [0, 1, 2, 3]
    TRB = [4, 5]
    rot.update({"kfb": 0, "gb": 0, "yt": 0, "tr": 0, "ost": 0, "ss": 0})

    class G:
        pass
    GX = G()
    GX.gi, GX.n, GX.ntc, GX.tc0, GX.tw, GX.npair, GX.nfh, GX.t0, GX.ti0 = 0, SEQ, 16, 2, 512, 4, 16, CTX, 1
    GX.FT, GX.GT, GX.KFD = FT_x, GT_x, KFD_x
    GX.YA = HTflat[:, 0:16384].rearrange("p (k d) -> p k d", d=D)
    GX.YB = BIGhi[:, 0:16384].rearrange("p (k d) -> p k d", d=D)
    GC = G()
    GC.gi, GC.n, GC.ntc, GC.tc0, GC.tw, GC.npair, GC.nfh, GC.t0, GC.ti0 = 1, CTX, 2, 0, 256, 1, 2, 0, 0
    GC.FT, GC.GT, GC.KFD = FT_c, GT_c, KFD_c
    GC.YA = HTflat[:, 16384:18432].rearrange("p (k d) -> p k d", d=D)
    GC.YB = BIGhi[:, 16384:18432].rearrange("p (k d) -> p k d", d=D)

    def load_F(g, i):
        if g.gi == 1:
            return FCA, FCB
        p.dma("sp", XINb[0][:, :, :], g.FT[:, i * 512:(i + 1) * 512].rearrange("(k p) f -> p k f", p=128), w=[("FA",)])
        p.dma("sp", XINb[1][:, :, :], g.FT[:, SEQ + i * 512:SEQ + (i + 1) * 512].rearrange("(k p) f -> p k f", p=128), w=[("FB",)])
        return XINb[0], XINb[1]

    def fwd_banks(g, FA, FB, srcA, srcB, c, tokA, tokB):
        bA = SS4[nxt("ss", 4)]
        bB = SS4[nxt("ss", 4)]
        for (bk, F, src, ftok, stok) in ((bA, FA, srcA, ("FA",), tokA), (bB, FB, srcB, ("FB",), tokB)):
            for tc in range(g.ntc):
                mm(PS[bk][:, :g.tw], src[:, g.tc0 + tc, c * 128:(c + 1) * 128], F[:, tc, :g.tw], tc == 0, tc == g.ntc - 1,
                   r=[ftok, ("FC",), stok], w=[("PS", bk)])
        return bA, bB

    def transposes_to(src_ap_fn, nblk, dst3, rtoks, wtok, eng="act"):
        trb = TRB[nxt("tr", 2)]
        for blk in range(nblk):
            p.op("pe", lambda e, trb=trb, blk=blk: e.transpose(PSbf[trb][:, blk * 128:(blk + 1) * 128], src_ap_fn(blk), IDENT[:, :]),
                 r=rtoks + [("IDENT",)], w=[("PS", trb)])
        src = PSbf[trb][:, 0:nblk * 128].rearrange("p (k d) -> p k d", d=128)
        if eng == "act":
            act(dst3, src, AF.Copy, r=[("PS", trb)], w=[wtok])
        else:
            p.op("dve", lambda e: e.tensor_copy(dst3, src), r=[("PS", trb)], w=[wtok])

    def filters(g, zT, dec):
        set_pj([6, 7])
        n = g.n
        HTf = HTflat.bitcast(F32)
        ZT = HTf[0:33, 0:n]
        H1 = HTf[0:64, 2048:2048 + n]
        H2 = HTf[0:64, 4096:4096 + n]
        W3o = HTf[0:64, 6144:8192].rearrange("p (d c) -> p d c", c=D)
        W1 = HTf[0:33, 8192:8256]
        W2 = HTf[0:64, 8256:8320]
        FSM = HTf[0:64, 8320:8323]
        BROW = SQ[:, :, :].rearrange("p c t -> p (c t)").bitcast(F32)[0:1, 0:2048]
        p.dma("sp", ZT, zT[:, :], w=[("ZT",)])
        p.dma("sp", W1, f_w1[:, :], w=[("FW",)])
        p.dma("sp", W2, f_w2[:, :], w=[("FW",)])
        p.dma("sp", FSM, f_sm[:, :], w=[("FW",)])
        p.dma("sp", BROW, f_bias[:, :], w=[("BROW",)])
        TWO_PI = 2.0 * math.pi
        tw = min(512, n)
        for (W, src, dst, bcol, kdim, stok, dtok) in ((W1, ZT, H1, 0, 33, ("ZT",), ("H1",)), (W2, H1, H2, 1, 64, ("H1",), ("H2",))):
            for i in range(n // tw):
                bk = pjb()
                mm(PS[bk][0:64, :tw], W, src[:, i * tw:(i + 1) * tw], True, True, r=[("FW",), stok], w=[("PS", bk)])
                k = nxt("tmp", 4)
                tsc(TMP[k][0:64, :tw], PS[bk][0:64, :tw], FSM[:, bcol:bcol + 1], FSM[:, 2:3], ALU.add, ALU.mult,
                    r=[("PS", bk), ("FW",)], w=[("TMP", k)])
                k2 = nxt("tmp", 4)
                MAGIC = 12582912.0
                tsc(TMP[k2][0:64, :tw], TMP[k][0:64, :tw], 1.0 / TWO_PI, MAGIC, ALU.mult, ALU.add, r=[("TMP", k)], w=[("TMP", k2)])
                tsc(TMP[k2][0:64, :tw], TMP[k2][0:64, :tw], -MAGIC, None, ALU.add, None, r=[("TMP", k2)], w=[("TMP", k2)])
                stt(TMP[k][0:64, :tw], TMP[k2][0:64, :tw], -TWO_PI, TMP[k][0:64, :tw], ALU.mult, ALU.add,
                    r=[("TMP", k), ("TMP", k2)], w=[("TMP", k)])
                tsc(TMP[k][0:64, :tw], TMP[k][0:64, :tw], 3.1415925, -3.1415925, ALU.min, ALU.max, r=[("TMP", k)], w=[("TMP", k)])
                act(dst[:, i * tw:(i + 1) * tw], TMP[k][0:64, :tw], AF.Sin, r=[("TMP", k)], w=[dtok])
        for o in range(2):
            p.dma("sp", W3o[:, 0, :], f_w3[:, o * D:(o + 1) * D], w=[("W3",)])
            p.dma("sp", W3o[:, 1, :], f_w3[:, 2 * D + o * D:2 * D + (o + 1) * D], w=[("W3",)])
            for tc in range(g.ntc):
                for half in range(2):
                    di = nxt("rs", 2)
                    p.dma("sp", RS[di][:, :], dec[tc * 128:(tc + 1) * 128, half * 512:(half + 1) * 512], w=[("RS", di)])
                    bF = SS4[nxt("ss", 4)]
                    bBk = SS4[nxt("ss", 4)]
                    mm(PS[bF][:, :], H2[:, tc * 128:(tc + 1) * 128], W3o[:, 0, half * 512:(half + 1) * 512], True, True,
                       r=[("H2",), ("W3",)], w=[("PS", bF)])
                    mm(PS[bBk][:, :], H2[:, tc * 128:(tc + 1) * 128], W3o[:, 1, half * 512:(half + 1) * 512], True, True,
                       r=[("H2",), ("W3",)], w=[("PS", bBk)])
                    k1 = nxt("tmp", 4)
                    k2 = nxt("tmp", 4)
                    tt(TMP[k1][:, :], PS[bF][:, :], RS[di][:, :], ALU.mult, r=[("PS", bF), ("RS", di)], w=[("TMP", k1)])
                    tt(TMP[k2][:, :], PS[bBk][:, :], RS[di][:, :], ALU.mult, r=[("PS", bBk), ("RS", di)], w=[("TMP", k2)])
                    if tc == 0:
                        tt(TMP[k1][0:1, :], TMP[k1][0:1, :], BROW[0:1, o * D + half * 512:o * D + (half + 1) * 512], ALU.add,
                           r=[("TMP", k1), ("BROW",)], w=[("TMP", k1)])
                    tt(Vtok[:, g.tc0 + tc, half * 512:(half + 1) * 512], TMP[k1][:, :], TMP[k2][:, :], ALU.add,
                       r=[("TMP", k1), ("TMP", k2)], w=[("KS",)])
                    tt(KDtok[:, g.tc0 + tc, half * 512:(half + 1) * 512], TMP[k1][:, :], TMP[k2][:, :], ALU.subtract,
                       r=[("TMP", k1), ("TMP", k2)], w=[("KD",)])
            for i in range(g.npair):
                FA, FB = load_F(g, i)
                for c in range(8):
                    bA, bB = fwd_banks(g, FA, FB, Vtok, KDtok, c, ("KS",), ("KD",))
                    k1 = nxt("tmp", 4)
                    k2 = nxt("tmp", 4)
                    act(TMP[k1][:, :g.tw], PS[bA][:, :g.tw], AF.Copy, r=[("PS", bA)], w=[("TMP", k1)])
                    p.op("dve", lambda e, k2=k2, bB=bB: e.tensor_copy(TMP[k2][:, :g.tw], PS[bB][:, :g.tw]), r=[("PS", bB)], w=[("TMP", k2)])
                    if i == 0:
                        p.op("dve", lambda e, k2=k2: e.memset(TMP[k2][:, 0:1], 0.0), r=[("TMP", k2)], w=[("TMP", k2)])
                        bN = pjb()
                        for tc in range(g.ntc):
                            mm(PS[bN][:, 0:1], Vtok[:, g.tc0 + tc, c * 128:(c + 1) * 128], FB[:, tc, 0:1], tc == 0, tc == g.ntc - 1,
                               r=[("FB",), ("FC",), ("KS",)], w=[("PS", bN)])
                        p.op("dve", lambda e, bN=bN, o=o, c=c: e.tensor_copy(KN[:, g.gi, o, c:c + 1], PS[bN][:, 0:1]),
                             r=[("PS", bN)], w=[("KN",)])
                    p.dma("sp", g.KFD[o, 0, c, :, i * g.tw:(i + 1) * g.tw], TMP[k1][:, :g.tw], r=[("TMP", k1)], w=[("KFD",)])
                    p.dma("sp", g.KFD[o, 1, c, :, i * g.tw:(i + 1) * g.tw], TMP[k2][:, :g.tw], r=[("TMP", k2)], w=[("KFD",)])
            p.barrier(full=False)

    def hy_fwd(g, order):
        nblk = g.tw // 128
        pend = [None]
        for i in range(g.npair):
            FA, FB = load_F(g, i)
            for c in range(8):
                kp = nxt("kfb", 4)
                kq = nxt("kfb", 4)
                p.dma("sp", KFB[kp][:, :g.tw], g.KFD[order, 0, c, :, i * g.tw:(i + 1) * g.tw], w=[("KFB", kp)])
                p.dma("sp", KFB[kq][:, :g.tw], g.KFD[order, 1, c, :, i * g.tw:(i + 1) * g.tw], w=[("KFB", kq)])
                bA, bB = fwd_banks(g, FA, FB, Vtok, Vtok, c, ("VT", g.gi), ("VT", g.gi))
                yi = nxt("yt", 2)
                ks = [nxt("tmp", 4) for _ in range(4)]
                tw = g.tw
                tt(TMP[ks[0]][:, :tw], PS[bA][:, :tw], KFB[kp][:, :tw], ALU.mult, r=[("PS", bA), ("KFB", kp)], w=[("TMP", ks[0])])
                tt(TMP[ks[1]][:, :tw], PS[bB][:, :tw], KFB[kq][:, :tw], ALU.mult, r=[("PS", bB), ("KFB", kq)], w=[("TMP", ks[1])])
                tt(YT[yi][:, 0, :tw], TMP[ks[0]][:, :tw], TMP[ks[1]][:, :tw], ALU.subtract, r=[("TMP", ks[0]), ("TMP", ks[1])], w=[("YT", yi, 0)])
                tt(TMP[ks[2]][:, :tw], PS[bA][:, :tw], KFB[kq][:, :tw], ALU.mult, r=[("PS", bA), ("KFB", kq)], w=[("TMP", ks[2])])
                tt(TMP[ks[3]][:, :tw], PS[bB][:, :tw], KFB[kp][:, :tw], ALU.mult, r=[("PS", bB), ("KFB", kp)], w=[("TMP", ks[3])])
                tt(YT[yi][:, 1, :tw], TMP[ks[2]][:, :tw], TMP[ks[3]][:, :tw], ALU.add, r=[("TMP", ks[2]), ("TMP", ks[3])], w=[("YT", yi, 1)])
                if i == 0:
                    tsc(YT[yi][:, 1, 0:1], PS[bB][:, 0:1], KN[:, g.gi, order, c:c + 1], None, ALU.mult, None,
                        r=[("PS", bB), ("KN",), ("YT", yi, 1)], w=[("YT", yi, 1)])
                def trs(yi=yi, i=i, c=c):
                    for half, Y in ((0, g.YA), (1, g.YB)):
                        transposes_to(lambda blk, yi=yi, half=half: YT[yi][:, half, blk * 128:(blk + 1) * 128], nblk,
                                      Y[:, i * nblk:(i + 1) * nblk, c * 128:(c + 1) * 128], [("YT", yi, half)], ("YF", g.gi, half, c))
                if pend[0] is not None:
                    pend[0]()
                pend[0] = trs
        if pend[0] is not None:
            pend[0]()

    def hy_inv(g, conv):
        pend = [None]
        ttw = g.tw
        nblk = ttw // 128
        ntt = g.n // ttw
        for j in range(ntt):
            if g.gi == 0:
                p.dma("sp", XINb[0][:, :, :], g.GT[0:SEQ, j * 512:(j + 1) * 512].rearrange("(k p) t -> p k t", p=128), w=[("FA",)])
                p.dma("sp", XINb[1][:, :, :], g.GT[SEQ:2 * SEQ, j * 512:(j + 1) * 512].rearrange("(k p) t -> p k t", p=128), w=[("FB",)])
                GA, GBt = XINb[0], XINb[1]
            else:
                GA, GBt = GCT[:, 0:2, :], GCT[:, 2:4, :]
            tk0 = g.t0 + j * ttw
            ti = g.ti0 + j
            for c in range(8):
                ga = nxt("gb", 4)
                p.dma("sp", GB[ga][:, :ttw], GD[conv, c, :, tk0:tk0 + ttw], r=[("GD", conv, c)], w=[("GB", ga)])
                if conv == 1:
                    gs = nxt("gb", 4)
                    p.dma("sp", GB[gs][:, :ttw], GD[2, c, :, tk0:tk0 + ttw], r=[("GD", 2, c)], w=[("GB", gs)])
                bk = SS4[nxt("ss", 4)]
                for half, (Y, Gt, ftok) in enumerate(((g.YA, GA, ("FA",)), (g.YB, GBt, ("FB",)))):
                    for fc in range(g.nfh):
                        mm(PS[bk][:, :ttw], Y[:, fc, c * 128:(c + 1) * 128], Gt[:, fc, :ttw], half == 0 and fc == 0, half == 1 and fc == g.nfh - 1,
                           r=[("YF", g.gi, half, c), ftok, ("FC",)], w=[("PS", bk)])
                if conv == 0:
                    yi = nxt("yt", 2)
                    tt(YT[yi][:, 0, :ttw], PS[bk][:, :ttw], GB[ga][:, :ttw], ALU.mult, r=[("PS", bk), ("GB", ga)], w=[("YT", yi, 0)])
                    def trs(yi=yi, j=j, c=c):
                        transposes_to(lambda blk, yi=yi: YT[yi][:, 0, blk * 128:(blk + 1) * 128], nblk,
                                      Vtok[:, g.tc0 + j * nblk:g.tc0 + (j + 1) * nblk, c * 128:(c + 1) * 128], [("YT", yi, 0)], ("VT", g.gi))
                    if pend[0] is not None:
                        pend[0]()
                    pend[0] = trs
                else:
                    k = nxt("tmp", 4)
                    tt(TMP[k][:, :ttw], PS[bk][:, :ttw], GB[ga][:, :ttw], ALU.mult, r=[("PS", bk), ("GB", ga)], w=[("TMP", k)])
                    tt(BIG[:, c, tk0:tk0 + ttw], TMP[k][:, :ttw], GB[gs][:, :ttw], ALU.mult, r=[("TMP", k), ("GB", gs)], w=[("G", c, ti)])

        if pend[0] is not None:
            pend[0]()

    def hyena(l, b, with_ctx):
        geoms = [GX] + ([GC] if with_ctx else [])
        set_pj([6, 7, 0, 1, 2, 3])
        RAWS = [BIGhi[:, 0:2308], BIGhi[:, 6 * NT:6 * NT + 2308]]
        for ri in range(2):
            for col in (0, 257, 258, 2307):
                p.op("dve", lambda e, col=col, ri=ri: e.memset(RAWS[ri][:, col:col + 1], 0.0), w=[("RAW", ri)])
        for pc in range(8):
            wi = nxt("wb", 2)
            load_w(WB[wi][:, :, 0:512], c_w_in[0][:, pc * 512:(pc + 1) * 512].rearrange("(c p) m -> p c m", p=128), ("WB", wi))
            for mi in range(4):
                m = pc * 4 + mi
                kind, c = m // 8, m % 8
                oi = nxt("ost", 4)
                OST = BIG[:, 10 + oi, :]
                ri = m % 2
                RAW = RAWS[ri]
                for ti, (t0, n, is_ctx) in enumerate(TILES):
                    if is_ctx and not with_ctx:
                        continue
                    bk = pjb()
                    for kc in range(8):
                        mm(PS[bk][:, :n], WB[wi][:, kc, mi * 128:(mi + 1) * 128], HT[:, kc, t0:t0 + n], kc == 0, kc == 7,
                           r=[("WB", wi), ("HT", kc, ti)], w=[("PS", bk)])
                    if kind == 3:
                        act(OST[:, t0:t0 + n], PS[bk][:, :n], AF.Silu, r=[("PS", bk)], w=[("OST", oi)])
                    else:
                        s0 = t0 + (1 if is_ctx else 3)
                        act(RAW[:, s0:s0 + n], PS[bk][:, :n], AF.Copy, r=[("PS", bk)], w=[("RAW", ri)])
                if kind < 3:
                    for ti, (t0, n, is_ctx) in enumerate(TILES):
                        if is_ctx and not with_ctx:
                            continue
                        s0 = t0 + (1 if is_ctx else 3)
                        k = nxt("tmp", 4)
                        tsc(TMP[k][:, :n], RAW[:, s0:s0 + n], CONVW[:, m, 1:2], CONVW[:, m, 3:4], ALU.mult, ALU.add,
                            r=[("RAW", ri), ("CONVW",)], w=[("TMP", k)], eng="pool")
                        stt(TMP[k][:, :n], RAW[:, s0 - 1:s0 - 1 + n], CONVW[:, m, 0:1], TMP[k][:, :n], ALU.mult, ALU.add,
                            r=[("RAW", ri), ("TMP", k), ("CONVW",)], w=[("TMP", k)])
                        stt(OST[:, t0:t0 + n], RAW[:, s0 + 1:s0 + 1 + n], CONVW[:, m, 2:3], TMP[k][:, :n], ALU.mult, ALU.add,
                            r=[("RAW", ri), ("TMP", k), ("CONVW",)], w=[("OST", oi)])
                if kind == 0:
                    for g in geoms:
                        for tcg in range(0, g.ntc, 4):
                            nb_ = min(4, g.ntc - tcg)
                            transposes_to(lambda blk, g=g, tcg=tcg, OST=OST: OST[:, (g.tc0 + tcg + blk) * 128:(g.tc0 + tcg + blk + 1) * 128], nb_,
                                          Vtok[:, g.tc0 + tcg:g.tc0 + tcg + nb_, c * 128:(c + 1) * 128], [("OST", oi)], ("VT", g.gi), eng="dve")
                else:
                    p.dma("sp", GD[kind - 1, c], OST[:, :], r=[("OST", oi)], w=[("GD", kind - 1, c)])
        p.barrier()
        for conv in range(2):
            for g in geoms:
                hy_fwd(g, conv)
            p.barrier()
            for g in geoms:
                hy_inv(g, conv)
            p.barrier()
        post(l, b, lambda kc, t0, n: BIG[:, kc, t0:t0 + n], lambda kc, ti: ("G", kc, ti), c_w_out[0], with_ctx, 8)

    if NL > 2:
        filters(GX, zT_x, dec_x)
        filters(GC, zT_c, dec_c)

    for b in range(NB):
        for l in range(NL):
            kind = l % 3
            with_ctx = l < DEPTH - 1
            if l == 0:
                prenorm(l, b)
            if kind == 0:
                mla(l, l // 3, b, with_ctx)
            elif kind == 1:
                swa(l, b, with_ctx)
            else:
                hyena(l, b, with_ctx)
            p.barrier(full=False)
    p.barrier()
    p.emit(es)
    es.close()
    return nc


def _rope_tables():
    rows = SEQ // 64
    row = np.repeat(np.arange(rows, dtype=np.float32), 64)
    col = np.tile(np.arange(64, dtype=np.float32), rows)
    per_axis = 32
    inv = (10000.0 ** (-np.arange(0, per_axis, 2, dtype=np.float32) / per_axis)).astype(np.float32)
    ang = np.concatenate([row[:, None] * inv, col[:, None] * inv], axis=-1).astype(np.float32)
    cos = np.cos(ang).astype(np.float32)
    sin = np.sin(ang).astype(np.float32)
    cosT = np.zeros((128, SEQ), np.float32)
    sinT = np.zeros((128, SEQ), np.float32)
    for pidx in range(128):
        d = pidx % 64
        cosT[pidx] = cos[:, d % 32]
        sinT[pidx] = -sin[:, d] if d < 32 else sin[:, d - 32]
    return cosT, sinT


def _colT(v, nch):
    L = v.shape[0]
    return np.ascontiguousarray(v.reshape(L, nch, 128).transpose(0, 2, 1))


PERM64 = np.concatenate([np.arange(32, 64), np.arange(0, 32)])


def prep_inputs(inp, NB, ncores):
    f = lambda a: np.ascontiguousarray(np.asarray(a, dtype=np.float32))
    x = f(inp["x"])
    ctx = f(inp["ctx"])
    c = f(inp["c"])
    c_ctx = f(inp["c_ctx"])
    cosT, sinT = _rope_tables()
    a_w_in = f(inp["a_w_in"])
    kr = a_w_in[:, :, 768:832]
    a_w_in2 = np.concatenate([a_w_in[:, :, 0:768], kr, kr[:, :, PERM64], a_w_in[:, :, 832:1856]], axis=2)
    a_w_q = f(inp["a_w_q"]).reshape(2, 512, 8, 192)
    a_w_q2 = np.concatenate([a_w_q, a_w_q[:, :, :, 128 + PERM64]], axis=3).reshape(2, 512, 8 * 256)
    b_w_in = f(inp["b_w_in"])[0]
    wA = []
    wB = []
    for hk in range(4):
        q = b_w_in[:, hk * 256:(hk + 1) * 256]
        qp = q.reshape(D, 4, 64)[:, :, PERM64].reshape(D, 256)
        k = b_w_in[:, 1024 + hk * 64:1024 + (hk + 1) * 64]
        kp = k[:, PERM64]
        v = b_w_in[:, 1280 + hk * 64:1280 + (hk + 1) * 64]
        g = b_w_in[:, 1536 + hk * 256:1536 + (hk + 1) * 256]
        wA.append(np.concatenate([q, qp], axis=1))
        wB.append(np.concatenate([k, k, kp, kp, v, g], axis=1))
    sink = f(inp["b_sink"])[0]
    sinkT = np.zeros((128, 8), np.float32)
    for pidx in range(128):
        for cc_ in range(8):
            sinkT[pidx, cc_] = sink[2 * cc_ + (1 if pidx >= 64 else 0)]
    ki = np.arange(128)[:, None]
    qi = np.arange(128)[None, :]
    maskT = np.concatenate([(ki <= qi), np.ones((128, 128), bool), (qi <= ki)], axis=1).astype(np.float32)
    def dft_mats(n):
        N2 = 2 * n
        t = np.arange(n, dtype=np.int64)[:, None]
        k = np.arange(n, dtype=np.int64)[None, :]
        ang = 2.0 * np.pi * ((t * k) % N2).astype(np.float64) / N2
        FT = np.zeros((n, N2), np.float64)
        FT[:, :n] = np.cos(ang)
        FT[:, n:] = -np.sin(ang)
        FT[:, n] = np.where(np.arange(n) % 2 == 0, 1.0, -1.0)
        GT = np.zeros((N2, n), np.float64)
        GT[:n, :] = (2.0 / N2) * np.cos(ang.T)
        GT[0, :] = 1.0 / N2
        GT[n:, :] = -(2.0 / N2) * np.sin(ang.T)
        GT[n, :] = np.where(np.arange(n) % 2 == 0, 1.0, -1.0) / N2
        return FT.astype(ml_dtypes.bfloat16), GT.astype(ml_dtypes.bfloat16)

    def pos_feats(n):
        t = np.linspace(0.0, 1.0, n, dtype=np.float32)[:, None]
        w = (np.float32(2.0 * math.pi / n) * np.arange(n, dtype=np.float32))[:, None]
        bands = np.linspace(1e-4, 15, 16, dtype=np.float32)[None, :]
        z = np.concatenate([t, np.cos(bands * w), -np.sin(bands * w)], axis=-1).astype(np.float32)
        cmin = math.log(1e-2) / 1.5
        cmax = math.log(1e-2) / 0.3
        deltas = np.abs(np.linspace(cmin, cmax, D, dtype=np.float32))
        dec = np.exp(-t * deltas[None, :]).astype(np.float32)
        return np.ascontiguousarray(z.T), np.ascontiguousarray(dec)

    FT_x, GT_x = dft_mats(SEQ)
    FT_c, GT_c = dft_mats(CTX)
    zT_x, dec_x = pos_feats(SEQ)
    zT_c, dec_c = pos_feats(CTX)
    cw = f(inp["c_conv_w"])[0]
    cb = f(inp["c_conv_b"])[0]
    convT = np.ascontiguousarray(np.concatenate([cw, cb[None, :]], axis=0).T.reshape(24, 128, 4).transpose(1, 0, 2))
    f_sm = np.ascontiguousarray(np.stack([f(inp["c_f_b1"])[0], f(inp["c_f_b2"])[0], f(inp["c_f_freq"])[0]], axis=1))
    shared = {
        "c_w_in": f(inp["c_w_in"]),
        "c_w_out": f(inp["c_w_out"]),
        "convT": convT,
        "f_w1": f(inp["c_f_w1"])[0],
        "f_w2": f(inp["c_f_w2"])[0],
        "f_w3": f(inp["c_f_w3"])[0],
        "f_sm": f_sm,
        "f_bias": np.ascontiguousarray(f(inp["c_filt_bias"])[0].reshape(1, 2048)),
        "zT_x": zT_x, "zT_c": zT_c, "dec_x": dec_x, "dec_c": dec_c,
        "FT_x": FT_x, "GT_x": GT_x, "FT_c": FT_c, "GT_c": GT_c,
        "identT": np.eye(128, dtype=np.float32),
        "b_wA": np.ascontiguousarray(np.stack(wA)),
        "b_wB": np.ascontiguousarray(np.stack(wB)),
        "b_w_out": f(inp["b_w_out"]),
        "sinkT": sinkT,
        "maskT": maskT,
        "w_mod": f(inp["w_mod"]),
        "b_modT": _colT(f(inp["b_mod"]), 24),
        "g_preT": _colT(f(inp["g_pre"]), 8),
        "g_postT": _colT(f(inp["g_post"]), 8),
        "a_w_in": np.ascontiguousarray(a_w_in2),
        "a_g_qT": _colT(f(inp["a_g_q"]), 4),
        "a_w_q": np.ascontiguousarray(a_w_q2),
        "a_g_kvT": _colT(f(inp["a_g_kv"]), 2),
        "a_w_kv": f(inp["a_w_kv"]),
        "a_w_out": f(inp["a_w_out"]),
        "cosT": cosT,
        "sinT": sinT,
    }
    maps = []
    for k in range(ncores):
        bs = slice(k * NB, (k + 1) * NB)
        xT = np.ascontiguousarray(x[bs].transpose(0, 2, 1)).reshape(NB, 8, 128, SEQ)
        cxT = np.ascontiguousarray(ctx[bs].transpose(0, 2, 1)).reshape(NB, 8, 128, CTX)
        cc = np.concatenate([c[bs], c_ctx[None, :]], axis=0)
        cT = np.ascontiguousarray(cc.T.reshape(8, 128, NB + 1).transpose(1, 0, 2))
        m = dict(shared)
        m.update({"xT": xT, "ctxT": cxT, "cT": cT})
        maps.append(m)
    return maps


def run(inp, NB, NL, ncores, trace=False):
    nc = build(NB, NL)
    maps = prep_inputs(inp, NB, ncores)
    res = run_bass_kernel_spmd(nc, maps, core_ids=list(range(ncores)), **({"trace": True} if trace else {}))
    outs = []
    for k in range(ncores):
        o = res.results[k]["outT"]
        outs.append(o.reshape(NB, D, SEQ).transpose(0, 2, 1))
    return np.ascontiguousarray(np.concatenate(outs, axis=0)), res


def kernel(**inputs):
    out, _ = run(inputs, BATCH // NCORES, DEPTH, NCORES)
    return out.astype(np.float32)
```
